# Optimizing a Trainium2 kernel written in Bass

```python
import math
import jax
import jax.numpy as jnp
from jax import lax
import numpy as np

D_MODEL = 1024
BATCH = 16
SEQ = 2048
DEPTH = 1

GRID_W = 64
CTX_LEN = 256
DA_HEADS = 8
DA_HEAD_DIM = 64
DA_V_DIM = 2 * DA_HEAD_DIM
DA_WIDTH = DA_HEADS * 2 * DA_HEAD_DIM
LRU_WIDTH = D_MODEL
LRU_BLOCKS = 16
LRU_BLOCK_DIM = LRU_WIDTH // LRU_BLOCKS
CONV_WIDTH = 4
CONV_PAD_LEFT = (CONV_WIDTH - 1) // 2
CONV_PAD_RIGHT = CONV_WIDTH - 1 - CONV_PAD_LEFT
LRU_C = 8.0
ROPE_THETA = 10000.0
Q_BLOCK = 128
NORM_EPS = 1e-6
N_BRANCHES = 2
IN_COLS = 4 * DA_WIDTH + 2 * LRU_WIDTH + N_BRANCHES * D_MODEL

kernel_name = 'hybrid_diffattn_rglru_prefix_dit_block'


def _rmsnorm(x, g):
    xf = x.astype(jnp.float32)
    y = xf * lax.rsqrt(jnp.mean(xf * xf, axis=-1, keepdims=True) + NORM_EPS)
    return (y * g.astype(jnp.float32)).astype(x.dtype)


def _lambda_init(layer_idx):
    return 0.8 - 0.6 * math.exp(-0.3 * layer_idx)


def _split_in(p):
    widths = (DA_WIDTH, DA_WIDTH, DA_WIDTH, DA_WIDTH, LRU_WIDTH, LRU_WIDTH, N_BRANCHES * D_MODEL)
    offsets = np.cumsum(widths)[:-1].tolist()
    return jnp.split(p, offsets, axis=-1)


def _axial_rope_tables(n_tokens):
    rows = n_tokens // GRID_W
    row = jnp.repeat(jnp.arange(rows, dtype=jnp.float32), GRID_W)
    col = jnp.tile(jnp.arange(GRID_W, dtype=jnp.float32), rows)
    axis_dim = DA_HEAD_DIM // 2
    inv_freq = ROPE_THETA ** (-jnp.arange(0, axis_dim, 2, dtype=jnp.float32) / axis_dim)
    ang_r = row[:, None] * inv_freq[None, :]
    ang_c = col[:, None] * inv_freq[None, :]
    return (jnp.cos(ang_r), jnp.sin(ang_r), jnp.cos(ang_c), jnp.sin(ang_c))


def _rotate(x, cos, sin):
    x1, x2 = jnp.split(x, 2, axis=-1)
    cs = cos[:, None, None, :]
    sn = sin[:, None, None, :]
    return jnp.concatenate([x1 * cs - x2 * sn, x1 * sn + x2 * cs], axis=-1)


def _apply_axial_rope(x, tables):
    cr, sr, cc, sc = tables
    half = DA_HEAD_DIM // 2
    xf = x.astype(jnp.float32)
    out = jnp.concatenate([_rotate(xf[..., :half], cr, sr), _rotate(xf[..., half:], cc, sc)], axis=-1)
    return out.astype(x.dtype)


def _diff_attend(q, k, v, lam):
    s = jnp.einsum('bqhjd,bkhjd->bhjqk', q, k, preferred_element_type=jnp.float32) * (DA_HEAD_DIM ** -0.5)
    p = jax.nn.softmax(s, axis=-1)
    w = p[:, :, 0] - lam * p[:, :, 1]
    return jnp.einsum('bhqk,bkhe->bqhe', w.astype(v.dtype), v)


def _latent_diff_attention(q_lat, k_all, v_all, lam):
    b, n = q_lat.shape[:2]
    n_blk = n // Q_BLOCK
    qb = q_lat.reshape(b, n_blk, Q_BLOCK, DA_HEADS, 2, DA_HEAD_DIM).swapaxes(0, 1)
    out = lax.map(lambda qq: _diff_attend(qq, k_all, v_all, lam), qb)
    return out.swapaxes(0, 1).reshape(b, n, DA_HEADS, DA_V_DIM)


def _diff_head_out(o, g_subln, lam_init):
    o = _rmsnorm(o, g_subln) * (1.0 - lam_init)
    return o.reshape(o.shape[0], o.shape[1], DA_WIDTH)


def _centred_dwconv(x, w, b):
    n = x.shape[1]
    xp = jnp.pad(x, ((0, 0), (CONV_PAD_LEFT, CONV_PAD_RIGHT), (0, 0)))
    y = b
    for j in range(CONV_WIDTH):
        y = y + xp[:, j:j + n] * w[j]
    return y


def _rglru_coeffs(xc, w_a, b_a, w_x, b_x, lam):
    b, n, _ = xc.shape
    xb = xc.reshape(b, n, LRU_BLOCKS, LRU_BLOCK_DIM)
    r = jax.nn.sigmoid((jnp.einsum('blni,nij->blnj', xb, w_a).reshape(b, n, LRU_WIDTH) + b_a).astype(jnp.float32))
    i = jax.nn.sigmoid((jnp.einsum('blni,nij->blnj', xb, w_x).reshape(b, n, LRU_WIDTH) + b_x).astype(jnp.float32))
    log_a = -LRU_C * r * jax.nn.softplus(-lam.astype(jnp.float32))
    a = jnp.exp(log_a)
    mult = jnp.sqrt(-jnp.expm1(2.0 * log_a))
    return a, mult * i * xc.astype(jnp.float32)


def _linear_scan(a, u, h0):
    def combine(e1, e2):
        return (e1[0] * e2[0], e2[0] * e1[1] + e2[1])
    a_cum, u_cum = lax.associative_scan(combine, (a, u), axis=1)
    return a_cum * h0[:, None, :] + u_cum


def _rglru_direction(xc_ctx, xc_lat, w_a, b_a, w_x, b_x, lam, reverse):
    a_c, u_c = _rglru_coeffs(xc_ctx, w_a, b_a, w_x, b_x, lam)
    a_l, u_l = _rglru_coeffs(xc_lat, w_a, b_a, w_x, b_x, lam)
    if reverse:
        a_c, u_c, a_l, u_l = (jnp.flip(a_c, 1), jnp.flip(u_c, 1), jnp.flip(a_l, 1), jnp.flip(u_l, 1))
    h_c = _linear_scan(a_c, u_c, jnp.zeros_like(u_c[:, 0]))
    h_l = _linear_scan(a_l, u_l, h_c[:, -1])
    if reverse:
        h_c, h_l = (jnp.flip(h_c, 1), jnp.flip(h_l, 1))
    return h_c, h_l


def _merge_branches(attn, lru, gate_attn, gate_lru, gate_merge, w_attn_out, w_lru_out, w_out):
    y_attn = (attn * jax.nn.silu(gate_attn)) @ w_attn_out
    y_lru = (lru * jax.nn.silu(gate_lru)) @ w_lru_out
    m_attn, m_lru = jnp.split(jax.nn.sigmoid(gate_merge), N_BRANCHES, axis=-1)
    return (m_attn * y_attn + m_lru * y_lru) @ w_out


def setup_inputs(seed: int = 0) -> dict:
    key = jax.random.key(seed)
    ks = jax.random.split(key, 24)
    f32 = jnp.float32

    def nrm(k, shape, scale):
        return jax.random.normal(k, shape, f32) * scale

    u = jax.random.uniform(ks[21], (DEPTH, 2, LRU_WIDTH), f32, 0.9, 0.999)
    a0 = u ** (1.0 / LRU_C)
    return {
        'x': nrm(ks[0], (BATCH, SEQ, D_MODEL), 1.0),
        'c': nrm(ks[1], (BATCH, D_MODEL), 1.0),
        'ctx': nrm(ks[2], (BATCH, CTX_LEN, D_MODEL), 1.0),
        'c_ctx': nrm(ks[3], (D_MODEL,), 1.0),
        'w_mod': nrm(ks[4], (DEPTH, D_MODEL, 3 * D_MODEL), D_MODEL ** -0.5),
        'b_mod': nrm(ks[5], (DEPTH, 3 * D_MODEL), 0.02),
        'g_pre': 1.0 + nrm(ks[6], (DEPTH, D_MODEL), 0.05),
        'g_post': 1.0 + nrm(ks[7], (DEPTH, D_MODEL), 0.05),
        'w_in': nrm(ks[8], (DEPTH, D_MODEL, IN_COLS), D_MODEL ** -0.5),
        'lambda_q1': nrm(ks[9], (DEPTH, DA_HEAD_DIM), 0.1),
        'lambda_k1': nrm(ks[10], (DEPTH, DA_HEAD_DIM), 0.1),
        'lambda_q2': nrm(ks[11], (DEPTH, DA_HEAD_DIM), 0.1),
        'lambda_k2': nrm(ks[12], (DEPTH, DA_HEAD_DIM), 0.1),
        'g_subln': 1.0 + nrm(ks[13], (DEPTH, DA_V_DIM), 0.05),
        'w_attn_out': nrm(ks[14], (DEPTH, DA_WIDTH, D_MODEL), DA_WIDTH ** -0.5),
        'conv_w': nrm(ks[15], (DEPTH, CONV_WIDTH, LRU_WIDTH), CONV_WIDTH ** -0.5),
        'conv_b': nrm(ks[16], (DEPTH, LRU_WIDTH), 0.02),
        'w_rg_a': nrm(ks[17], (DEPTH, 2, LRU_BLOCKS, LRU_BLOCK_DIM, LRU_BLOCK_DIM), LRU_BLOCK_DIM ** -0.5),
        'b_rg_a': nrm(ks[18], (DEPTH, 2, LRU_WIDTH), 0.02),
        'w_rg_x': nrm(ks[19], (DEPTH, 2, LRU_BLOCKS, LRU_BLOCK_DIM, LRU_BLOCK_DIM), LRU_BLOCK_DIM ** -0.5),
        'b_rg_x': nrm(ks[20], (DEPTH, 2, LRU_WIDTH), 0.02),
        'lru_lambda': jnp.log(a0) - jnp.log1p(-a0),
        'w_lru_out': nrm(ks[22], (DEPTH, LRU_WIDTH, D_MODEL), LRU_WIDTH ** -0.5),
        'w_out': nrm(ks[23], (DEPTH, D_MODEL, D_MODEL), D_MODEL ** -0.5),
    }


def reference(x, c, ctx, c_ctx, w_mod, b_mod, g_pre, g_post, w_in, lambda_q1, lambda_k1, lambda_q2, lambda_k2,
              g_subln, w_attn_out, conv_w, conv_b, w_rg_a, b_rg_a, w_rg_x, b_rg_x, lru_lambda, w_lru_out, w_out):
    b, n = x.shape[:2]
    nc = ctx.shape[1]
    rope = _axial_rope_tables(n)
    silu_c = jax.nn.silu(c)
    silu_cc = jax.nn.silu(c_ctx)
    for l in range(DEPTH):
        last = l == DEPTH - 1
        shift_l, scale_l, gate_l = jnp.split((silu_c @ w_mod[l] + b_mod[l])[:, None, :], 3, axis=-1)
        shift_c, scale_c, gate_c = jnp.split(silu_cc @ w_mod[l] + b_mod[l], 3, axis=-1)
        h = _rmsnorm(x, g_pre[l]) * (1.0 + scale_l) + shift_l
        hc = _rmsnorm(ctx, g_pre[l]) * (1.0 + scale_c) + shift_c
        q, k, v, ga, xr, gr, gm = _split_in(h @ w_in[l])
        qc, kc, vc, gac, xrc, grc, gmc = _split_in(hc @ w_in[l])
        q = _apply_axial_rope(q.reshape(b, n, DA_HEADS, 2, DA_HEAD_DIM), rope)
        k = _apply_axial_rope(k.reshape(b, n, DA_HEADS, 2, DA_HEAD_DIM), rope)
        v = v.reshape(b, n, DA_HEADS, DA_V_DIM)
        kc = kc.reshape(b, nc, DA_HEADS, 2, DA_HEAD_DIM)
        vc = vc.reshape(b, nc, DA_HEADS, DA_V_DIM)
        lam_init = _lambda_init(l)
        lam = (jnp.exp(jnp.sum(lambda_q1[l].astype(jnp.float32) * lambda_k1[l].astype(jnp.float32)))
               - jnp.exp(jnp.sum(lambda_q2[l].astype(jnp.float32) * lambda_k2[l].astype(jnp.float32))) + lam_init)
        k_all = jnp.concatenate([kc, k], axis=1)
        v_all = jnp.concatenate([vc, v], axis=1)
        attn_lat = _diff_head_out(_latent_diff_attention(q, k_all, v_all, lam), g_subln[l], lam_init)
        xr_l = _centred_dwconv(xr, conv_w[l], conv_b[l])
        xr_c = _centred_dwconv(xrc, conv_w[l], conv_b[l])
        hf_c, hf_l = _rglru_direction(xr_c, xr_l, w_rg_a[l, 0], b_rg_a[l, 0], w_rg_x[l, 0], b_rg_x[l, 0], lru_lambda[l, 0], False)
        hb_c, hb_l = _rglru_direction(xr_c, xr_l, w_rg_a[l, 1], b_rg_a[l, 1], w_rg_x[l, 1], b_rg_x[l, 1], lru_lambda[l, 1], True)
        lru_lat = (hf_l + hb_l).astype(x.dtype)
        y = _merge_branches(attn_lat, lru_lat, ga, gr, gm, w_attn_out[l], w_lru_out[l], w_out[l])
        x_new = x + gate_l * _rmsnorm(y, g_post[l])
        if not last:
            qc = qc.reshape(b, nc, DA_HEADS, 2, DA_HEAD_DIM)
            attn_ctx = _diff_head_out(_diff_attend(qc, kc, vc, lam), g_subln[l], lam_init)
            lru_ctx = (hf_c + hb_c).astype(ctx.dtype)
            yc = _merge_branches(attn_ctx, lru_ctx, gac, grc, gmc, w_attn_out[l], w_lru_out[l], w_out[l])
            ctx = ctx + gate_c * _rmsnorm(yc, g_post[l])
        x = x_new
    return x
```

```python
import math
from contextlib import ExitStack

import numpy as np
import concourse.bass as bass
import concourse.mybir as mybir
from concourse.bass_utils import run_bass_kernel_spmd

F32 = mybir.dt.float32
BF16 = mybir.dt.bfloat16
ALU = mybir.AluOpType
AF = mybir.ActivationFunctionType
AX = mybir.AxisListType

N_CORES = 8
NB = 2
D = 1024
L = 2048
LC = 256
T = L + LC
NH = 8
EPS = 1e-6
LAM_INIT = 0.8 - 0.6 * math.exp(0.0)
NS = 144
NR = 2432
W = 2310
C0 = 1
L0 = 260
ENGS = ("pe", "act", "dve", "pool", "sp")
DBG = {}


class Buf:
    __slots__ = ("name", "writers", "readers", "dsem", "dcount", "psum")

    def __init__(self, name):
        self.name = name
        self.psum = name.startswith("bank")
        self.writers = []
        self.readers = []
        self.dsem = None
        self.dcount = 0


class Op:
    __slots__ = ("eng", "fn", "deps", "signal", "tok", "is_dma", "pos")

    def __init__(self, eng, fn, is_dma=False):
        self.eng = eng
        self.fn = fn
        self.deps = []
        self.signal = False
        self.tok = None
        self.is_dma = is_dma


class Prog:
    def __init__(self, nc, stack):
        self.nc = nc
        self.stack = stack
        self.ops = {e: [] for e in ENGS}
        self.esem = {e: stack.enter_context(nc.semaphore("s_" + e)) for e in ENGS if e != "sp"}
        self.nsig = {e: 0 for e in ENGS}
        self.dma_bufs = []
        self.bar_deps = {e: [] for e in ENGS}
        self.dma_since_bar = []

    def _track(self, op, reads, writes, pwrites):
        deps = []
        for b in reads:
            deps.extend(b.writers)
            if b.psum:
                deps.extend(r for r in b.readers if r.eng != op.eng)
            b.readers.append(op)
        for b in writes:
            deps.extend(b.readers)
            deps.extend(b.writers)
            b.writers = [op]
            b.readers = []
        for b in pwrites:
            if b.readers:
                deps.extend(b.readers)
                b.writers = [op]
                b.readers = []
            else:
                b.writers.append(op)
        if self.bar_deps[op.eng]:
            deps.extend(self.bar_deps[op.eng])
            self.bar_deps[op.eng] = []
        op.deps = [d for d in deps if d is not op]

    def op(self, eng, fn, reads=(), writes=(), pwrites=()):
        o = Op(eng, fn)
        self._track(o, reads, writes, pwrites)
        self.ops[eng].append(o)
        return o

    def dma(self, eng, fn, reads=(), writes=(), pwrites=(), sem_buf=None):
        o = Op(eng, fn, is_dma=True)
        self._track(o, reads, writes, pwrites)
        b = sem_buf
        if b.dsem is None:
            b.dsem = self.stack.enter_context(self.nc.semaphore("d_" + b.name))
            self.dma_bufs.append(b)
        b.dcount += 1
        o.tok = (b.dsem, 16 * b.dcount)
        o.signal = True
        self.ops[eng].append(o)
        self.dma_since_bar.append(o)
        return o

    def barrier(self):
        last = []
        for e in ENGS:
            for o in reversed(self.ops[e]):
                if not o.is_dma:
                    last.append(o)
                    break
        last.extend(self.dma_since_bar)
        self.dma_since_bar = []
        for e in ENGS:
            self.bar_deps[e] = self.bar_deps[e] + list(last)

    def finalize(self):
        if getattr(self, "_finalized", False):
            return
        self._finalized = True
        for e in ENGS:
            for i, o in enumerate(self.ops[e]):
                o.pos = i
        for e in ENGS:
            for o in self.ops[e]:
                best = {}
                keep = []
                for d in o.deps:
                    if d.is_dma:
                        keep.append(d)
                        continue
                    if d.eng == "pe" and o.eng == "pe":
                        continue
                    if d.eng not in best or best[d.eng].pos < d.pos:
                        best[d.eng] = d
                for d in best.values():
                    d.signal = True
                    keep.append(d)
                o.deps = keep
        for e in ENGS:
            if e == "sp":
                continue
            n = 0
            for o in self.ops[e]:
                if o.is_dma:
                    continue
                if o.signal:
                    n += 1
                    o.tok = (self.esem[e], n)
            self.nsig[e] = n

    def emit(self):
        self.finalize()
        prog = self

        def run(ename):
            def body(e):
                known = {}
                for o in prog.ops[ename]:
                    need = {}
                    for d in o.deps:
                        if d.eng == "pe" and ename == "pe" and not d.is_dma:
                            continue
                        sem, val = d.tok
                        k = id(sem)
                        if known.get(k, 0) >= val:
                            continue
                        if k not in need or need[k][1] < val:
                            need[k] = (sem, val)
                    for k, (sem, val) in need.items():
                        e.wait_ge(sem, val)
                        known[k] = val
                    ins = o.fn(e)
                    if o.signal:
                        ins.then_inc(o.tok[0], 16 if o.is_dma else 1)
                if ename == "sp":
                    for b in prog.dma_bufs:
                        if known.get(id(b.dsem), 0) < 16 * b.dcount:
                            e.wait_ge(b.dsem, 16 * b.dcount)
                    for en in ENGS:
                        if en != "sp" and prog.nsig[en] > 0:
                            e.wait_ge(prog.esem[en], prog.nsig[en])
            return body

        with self.nc.Block() as block:
            block.tensor(run("pe"))
            block.scalar(run("act"))
            block.vector(run("dve"))
            block.gpsimd(run("pool"))
            block.sync(run("sp"))


def MM(out, lhsT, rhs, start, stop, **kw):
    return lambda e: e.matmul(out, lhsT=lhsT, rhs=rhs, start=start, stop=stop, **kw)


def TR(out, in_, ident):
    return lambda e: e.transpose(out=out, in_=in_, identity=ident)


def ACTF(out, in_, func, scale=1.0, bias=0.0, accum_out=None):
    if accum_out is None:
        return lambda e: e.activation(out=out, in_=in_, func=func, bias=bias, scale=scale)
    return lambda e: e.activation(out=out, in_=in_, func=func, bias=bias, scale=scale, accum_out=accum_out)


def TS(out, in0, s1, s2, op0, op1=None):
    if op1 is None:
        return lambda e: e.tensor_scalar(out=out, in0=in0, scalar1=s1, scalar2=None, op0=op0)
    return lambda e: e.tensor_scalar(out=out, in0=in0, scalar1=s1, scalar2=s2, op0=op0, op1=op1)


def STT(out, in0, scalar, in1, op0, op1):
    return lambda e: e.scalar_tensor_tensor(out=out, in0=in0, scalar=scalar, in1=in1, op0=op0, op1=op1)


def TT(out, in0, in1, op):
    return lambda e: e.tensor_tensor(out=out, in0=in0, in1=in1, op=op)


def CP(out, in_):
    return lambda e: e.tensor_copy(out=out, in_=in_)


def MSET(ap, val):
    return lambda e: e.memset(ap, val)


def SCAN(out, d0, d1, init):
    return lambda e: e.tensor_tensor_scan(out=out, data0=d0, data1=d1, initial=init, op0=ALU.mult, op1=ALU.add)


def DMA(out, in_):
    return lambda e: e.dma_start(out=out, in_=in_)


def RECIP(out, in_):
    return lambda e: e.reciprocal(out=out, in_=in_)


def RED(out, in_):
    return lambda e: e.tensor_reduce(out=out, in_=in_, axis=AX.X, op=ALU.add)


def build_program(nb=NB, stop_after=None, dbg=False):
    nc = bass.Bass("TRN2", target_bir_lowering=False)
    dr = lambda n, s, k="ExternalInput", dt=F32: nc.dram_tensor(n, list(s), dt, kind=k).ap()
    x_d = dr("x", [NB, L, D])
    ctx_d = dr("ctx", [NB, LC, D])
    smalls_d = dr("smalls", [128, NS])
    rows_d = dr("rows", [128, NR])
    wmodA_d = dr("wmodA", [16, 128, 1024])
    wmodG_d = dr("wmodG", [8, 2, 128, 512])
    wit_d = dr("wit", [8, 128, 6, 1024])
    wrg_d = dr("wrg", [8, 128, 512])
    wgm_d = dr("wgm", [16, 128, 1024])
    wao_d = dr("wao", [8, 128, 1024])
    wlo_d = dr("wlo", [8, 128, 1024])
    wout_d = dr("wout", [128, 8192])
    tabs_d = dr("tabs", [128, 4096])
    ident_d = dr("ident", [128, 128])
    perm_d = dr("perm", [128, 128])
    out_d = dr("out", [NB, L, D], k="ExternalOutput")
    dbg_d = {}
    if dbg:
        dbg_d["hT"] = dr("dbg_hT", [128, 8 * T], k="ExternalOutput", dt=BF16)
        dbg_d["lruG"] = dr("dbg_lruG", [128, 8 * L], k="ExternalOutput", dt=BF16)
        dbg_d["attnG"] = dr("dbg_attnG", [128, 8 * L], k="ExternalOutput", dt=BF16)

    with ExitStack() as st:
        P = Prog(nc, st)
        sb = lambda n, s, d: st.enter_context(nc.sbuf_tensor("sb_" + n, list(s), d))

        hT = sb("hT", [128, 8 * T], BF16)
        attnG = sb("attnG", [128, 8 * L], BF16)
        lruG = sb("lruG", [128, 8 * L], BF16)
        Gbc = sb("Gbc", [128, NB * D], F32)
        smalls = sb("smalls", [128, NS], F32)
        cst = sb("cst", [128, 512], F32)
        ident_f = sb("ident_f", [128, 128], F32)
        ident_b = sb("ident_b", [128, 128], BF16)
        perm_b = sb("perm_b", [128, 128], BF16)
        gsub_bc = sb("gsub_bc", [128, 128], F32)
        wstg = sb("wstg", [128, 3 * 1024], F32)
        wbf = sb("wbf", [128, 6 * 1024], BF16)
        rgstg = sb("rgstg", [128, 2 * 512], F32)
        rgbf = sb("rgbf", [128, 2 * 512], BF16)
        dg = sb("dg", [128, 512], F32)
        U = sb("U", [128, 15360], F32)
        UB = U[:, :].bitcast(BF16)
        ps = st.enter_context(nc.psum_tensor("ps", [128, 4096], F32))
        psb = ps[:, :].bitcast(BF16)

        B = {}

        def buf(name):
            if name not in B:
                B[name] = Buf(name)
            return B[name]

        bank_ap = [ps[:, i * 512:(i + 1) * 512] for i in range(8)]
        bank_b = [buf("bank%d" % i) for i in range(8)]
        Bsm, Bcst, Bidf, Bidb, Bpermb, Bgsub = buf("smalls"), buf("cst"), buf("idf"), buf("idb"), buf("permb"), buf("gsub")
        BG = [buf("G%d" % b) for b in range(NB)]
        BhT = [buf("hT%d" % i) for i in range(5)]
        Bwstg = [buf("wstg%d" % i) for i in range(3)]
        Bwbf = [buf("wbf%d" % i) for i in range(6)]
        Brgs = [buf("rgs%d" % i) for i in range(2)]
        Brgb = [buf("rgb%d" % i) for i in range(2)]

        C_SC4 = 0
        C_TMP = 32
        C_MOD = 64
        C_A = 128
        C_LAM = 152
        C_CN = 160
        C_H1 = 176
        C_HBA = 192
        C_HBX = 208
        C_NH = 224
        C_SSQ = 232
        C_RSTD = 240
        C_E = 248
        C_RS = 264
        C_SQ4 = 272
        C_RS4 = 276
        C_S12 = 280

        wctr = {"s": 0, "b": 0, "rs": 0, "rb": 0}

        def wchunk(src_ap, cast_eng="pool", defer=None):
            s = wctr["s"] % 3
            wctr["s"] += 1
            j = wctr["b"] % 6
            wctr["b"] += 1
            stg = wstg[:, s * 1024:(s + 1) * 1024]
            dst = wbf[:, j * 1024:(j + 1) * 1024]
            P.dma("sp", DMA(stg, src_ap), writes=[Bwstg[s]], sem_buf=Bwstg[s])
            cast = lambda: P.op(cast_eng, CP(dst, stg), reads=[Bwstg[s]], writes=[Bwbf[j]])
            if defer is None:
                cast()
            else:
                defer.append(cast)
            return dst, Bwbf[j]

        def wstage(src_ap, n=1024):
            s = wctr["s"] % 3
            wctr["s"] += 1
            stg = wstg[:, s * 1024:s * 1024 + n]
            P.dma(("sp", "act")[wctr["s"] % 2], DMA(stg, src_ap), writes=[Bwstg[s]], sem_buf=Bwstg[s])
            return stg, Bwstg[s]

        def rgchunk(c):
            s = wctr["rs"] % 2
            wctr["rs"] += 1
            stg = rgstg[:, s * 512:(s + 1) * 512]
            dst = rgbf[:, s * 512:(s + 1) * 512]
            P.dma("sp", DMA(stg, wrg_d[c]), writes=[Brgs[s]], sem_buf=Brgs[s])
            P.op("pool", CP(dst, stg), reads=[Brgs[s]], writes=[Brgb[s]])
            return dst, Brgb[s]

        PRE = {}
        bank_rr = {"i": 0}

        def next_bank(choices=(0, 1, 2, 3, 7)):
            i = choices[bank_rr["i"] % len(choices)]
            bank_rr["i"] += 1
            return i

        evac_rr = {"i": 0}

        P.dma("sp", DMA(smalls[:, :], smalls_d[:, :]), writes=[Bsm], sem_buf=Bsm)
        P.dma("sp", DMA(ident_f[:, :], ident_d[:, :]), writes=[Bidf], sem_buf=Bidf)
        Brows = buf("rows")
        rows = U[:, 0:NR]
        P.dma("sp", DMA(rows, rows_d[:, :]), writes=[Brows], sem_buf=Brows)
        Bpermf = buf("permf")
        permf = U[:, NR:NR + 128]
        P.dma("sp", DMA(permf, perm_d[:, :]), writes=[Bpermf], sem_buf=Bpermf)
        P.op("dve", CP(ident_b[:, :], ident_f[:, :]), reads=[Bidf], writes=[Bidb])
        P.op("dve", CP(perm_b[:, :], permf), reads=[Bpermf], writes=[Bpermb])
        P.op("act", ACTF(cst[:, C_TMP:C_TMP + 32], smalls[:, 0:32], AF.Tanh, scale=0.5), reads=[Bsm], writes=[Bcst])
        P.op("dve", STT(cst[:, C_SC4:C_SC4 + 32], cst[:, C_TMP:C_TMP + 32], 1.0, smalls[:, 0:32], ALU.add, ALU.mult),
             reads=[Bsm, Bcst], writes=[Bcst])
        P.op("dve", TS(cst[:, C_SC4:C_SC4 + 32], cst[:, C_SC4:C_SC4 + 32], 0.5, None, ALU.mult), reads=[Bcst], writes=[Bcst])
        P.op("dve", MSET(cst[:, C_NH:C_NH + 8], -0.5), writes=[Bcst])
        P.op("act", ACTF(cst[:, C_E:C_E + 16], smalls[:, 128:144], AF.Exp, scale=-1.0), reads=[Bsm], writes=[Bcst])
        P.op("act", ACTF(cst[:, C_E:C_E + 16], cst[:, C_E:C_E + 16], AF.Ln, bias=1.0), reads=[Bcst], writes=[Bcst])
        P.op("dve", TS(cst[:, C_CN:C_CN + 16], cst[:, C_E:C_E + 16], -8.0, None, ALU.mult), reads=[Bcst], writes=[Bcst])
        P.op("dve", TS(cst[:, C_H1:C_H1 + 16], cst[:, C_E:C_E + 16], -4.0, None, ALU.mult), reads=[Bcst], writes=[Bcst])
        P.op("dve", TS(cst[:, C_HBA:C_HBA + 16], smalls[:, 96:112], 0.5, None, ALU.mult), reads=[Bsm], writes=[Bcst])
        P.op("dve", TS(cst[:, C_HBX:C_HBX + 16], smalls[:, 112:128], 0.5, None, ALU.mult), reads=[Bsm], writes=[Bcst])
        ltmp = U[:, NR + 128:NR + 128 + 128]
        Bltmp = buf("ltmp")
        P.op("dve", TT(ltmp[:, 0:64], rows[:, 2176:2240], rows[:, 2240:2304], ALU.mult), reads=[Brows], writes=[Bltmp])
        P.op("dve", TT(ltmp[:, 64:128], rows[:, 2304:2368], rows[:, 2368:2432], ALU.mult), reads=[Brows, Bltmp], writes=[Bltmp])
        P.op("dve", RED(cst[:, C_S12:C_S12 + 1], ltmp[:, 0:64]), reads=[Bltmp], writes=[Bcst])
        P.op("dve", RED(cst[:, C_S12 + 1:C_S12 + 2], ltmp[:, 64:128]), reads=[Bltmp, Bcst], writes=[Bcst])
        P.op("act", ACTF(cst[:, C_S12:C_S12 + 2], cst[:, C_S12:C_S12 + 2], AF.Exp), reads=[Bcst], writes=[Bcst])
        P.op("dve", TT(cst[:, C_LAM:C_LAM + 1], cst[:, C_S12 + 1:C_S12 + 2], cst[:, C_S12:C_S12 + 1], ALU.subtract),
             reads=[Bcst], writes=[Bcst])
        P.op("dve", TS(cst[:, C_LAM:C_LAM + 1], cst[:, C_LAM:C_LAM + 1], -LAM_INIT, None, ALU.add), reads=[Bcst], writes=[Bcst])
        P.op("dve", TS(gsub_bc[:, :], rows[:, 2048:2176], (1.0 - LAM_INIT) * 0.5, None, ALU.mult), reads=[Brows], writes=[Bgsub])

        bi = 7
        for ch in range(16):
            stg, Bs = wstage(wmodA_d[ch])
            for kc in range(8):
                P.op("pe", MM(bank_ap[bi][:, ch * 4:(ch + 1) * 4], stg[:, kc * 128:(kc + 1) * 128],
                              cst[:, C_SC4 + kc * 4:C_SC4 + kc * 4 + 4], kc == 0, kc == 7),
                     reads=[Bs, Bcst], pwrites=[bank_b[bi]])
        for v in range(3):
            src = bank_ap[bi][:, 0:64].rearrange("p (c v) -> p c v", v=4)[:, :, v]
            dst = cst[:, C_MOD:C_MOD + 64].rearrange("p (c v) -> p c v", v=4)[:, :, v]
            P.op("dve", TT(dst, src, smalls[:, 32:48], ALU.add), reads=[bank_b[bi], Bsm, Bcst], writes=[Bcst])
        modv = cst[:, C_MOD:C_MOD + 64].rearrange("p (c v) -> p c v", v=4)
        for v in range(3):
            P.op("dve", STT(cst[:, C_A + v * 8:C_A + v * 8 + 8], modv[:, 8:16, v], 1.0, smalls[:, 48:56], ALU.add, ALU.mult),
                 reads=[Bcst, Bsm], writes=[Bcst])

        def A_col(v, kc):
            return cst[:, C_A + v * 8 + kc:C_A + v * 8 + kc + 1]

        def Sh_col(v, kc):
            return cst[:, C_MOD + kc * 4 + v:C_MOD + kc * 4 + v + 1]

        ones_f = U[:, 2816:2944]
        Bones = buf("ones")
        P.op("dve", MSET(ones_f, 1.0), writes=[Bones])
        scb = U[:, 3072:3072 + 16 * 128]
        Bscb = buf("scb")
        for b in range(NB):
            for kc in range(8):
                P.op("dve", TS(scb[:, (b * 8 + kc) * 128:(b * 8 + kc + 1) * 128], ones_f,
                               cst[:, C_SC4 + kc * 4 + b:C_SC4 + kc * 4 + b + 1], None, ALU.mult),
                     reads=[Bones, Bcst], pwrites=[Bscb])
        for kc in range(8):
            for hf in range(2):
                stg, Bs = wstage(wmodG_d[kc, hf], n=512)
                for b in range(NB):
                    bk = b * 2 + hf
                    P.op("pe", MM(bank_ap[bk], scb[:, (b * 8 + kc) * 128:(b * 8 + kc + 1) * 128], stg, kc == 0, kc == 7),
                         reads=[Bs, Bscb], pwrites=[bank_b[bk]])
        for b in range(NB):
            for hf in range(2):
                bk = b * 2 + hf
                g = Gbc[:, b * D + hf * 512:b * D + (hf + 1) * 512]
                P.op("dve", TT(g, bank_ap[bk], rows[:, 1024 + hf * 512:1024 + (hf + 1) * 512], ALU.add),
                     reads=[bank_b[bk], Brows], pwrites=[BG[b]])
                P.op("dve", TT(g, g, rows[:, hf * 512:(hf + 1) * 512], ALU.mult), reads=[BG[b], Brows], pwrites=[BG[b]])

        P.barrier()

        def phase_p1(b):
            ring = [U[:, i * 1024:(i + 1) * 1024] for i in range(8)]
            Bring = [buf("ring%d" % i) for i in range(8)]
            junk = UB[:, 16384:16384 + 1024]
            Bjunk = buf("junk")
            Bssq, Brstd = buf("ssq"), buf("rstd")
            slot = 0
            for g in range(5):
                n = 2 if g == 0 else 4
                v = 2 if g == 0 else b
                tok0 = 0 if g == 0 else LC + (g - 1) * 512
                slots = []
                for i in range(n):
                    s = slot % 8
                    slot += 1
                    slots.append(s)
                    src = ctx_d[b, i * 128:(i + 1) * 128, :] if g == 0 else x_d[b, (g - 1) * 512 + i * 128:(g - 1) * 512 + (i + 1) * 128, :]
                    P.dma(("sp", "act")[i % 2], DMA(ring[s], src), writes=[Bring[s]], sem_buf=Bring[s])
                for i, s in enumerate(slots):
                    P.op("act", ACTF(junk, ring[s], AF.Square, accum_out=cst[:, C_SSQ + i:C_SSQ + i + 1]),
                         reads=[Bring[s]], writes=[Bjunk], pwrites=[Bssq])
                P.op("pool", TS(cst[:, C_RSTD:C_RSTD + n], cst[:, C_SSQ:C_SSQ + n], 1.0 / D, EPS, ALU.mult, ALU.add),
                     reads=[Bssq], writes=[Brstd])
                P.op("pool", TT(cst[:, C_RSTD:C_RSTD + n], cst[:, C_RSTD:C_RSTD + n], cst[:, C_NH:C_NH + n], ALU.pow),
                     reads=[Brstd, Bcst], writes=[Brstd])
                for i, s in enumerate(slots):
                    P.op("dve", TS(ring[s], ring[s], cst[:, C_RSTD + i:C_RSTD + i + 1], None, ALU.mult),
                         reads=[Brstd, Bring[s]], writes=[Bring[s]])
                for kc in range(8):
                    bk = next_bank()
                    for i, s in enumerate(slots):
                        P.op("pe", TR(bank_ap[bk][:, i * 128:(i + 1) * 128], ring[s][:, kc * 128:(kc + 1) * 128], ident_f[:, :]),
                             reads=[Bring[s], Bidf], pwrites=[bank_b[bk]])
                    dst = hT[:, kc * T + tok0:kc * T + tok0 + n * 128]
                    src = bank_ap[bk][:, 0:n * 128]
                    if evac_rr["i"] % 2 == 0:
                        P.op("act", ACTF(dst, src, AF.Identity, scale=A_col(v, kc), bias=Sh_col(v, kc)),
                             reads=[bank_b[bk], Bcst], pwrites=[BhT[g]])
                    else:
                        P.op("dve", TS(dst, src, A_col(v, kc), Sh_col(v, kc), ALU.mult, ALU.add),
                             reads=[bank_b[bk], Bcst], pwrites=[BhT[g]])
                    evac_rr["i"] += 1

        def hT_blk(kc, g):
            tok0 = 0 if g == 0 else LC + (g - 1) * 512
            n = 256 if g == 0 else 512
            return hT[:, kc * T + tok0:kc * T + tok0 + n], n

        def proj_fm(w_ap, Bw, g, bk):
            for kc in range(8):
                rhs, n = hT_blk(kc, g)
                P.op("pe", MM(bank_ap[bk][:, 0:n], w_ap[:, kc * 128:(kc + 1) * 128], rhs, kc == 0, kc == 7),
                     reads=[Bw, BhT[g]], pwrites=[bank_b[bk]])
            return n

        def phase_lru(b):
            FB = [U[:, i * W:(i + 1) * W] for i in range(6)]
            BF = [buf("LF%d" % i) for i in range(6)]
            XCB = UB[:, 12 * W:13 * W]
            Bxcb = buf("XCB")
            free = [0, 1, 2, 3, 4, 5]
            lo, hi = C0, L0 + L
            MP = ps[:, 2 * 512:2 * 512 + W]
            Bmp = [bank_b[i] for i in range(2, 7)]
            LB = (0, 1, 7)
            blocks = []
            p0 = lo
            while p0 < hi:
                blocks.append((p0, min(512, hi - p0)))
                p0 += 512
            for i in range(6):
                P.op("pool", MSET(FB[i][:, 0:1], 0.0), pwrites=[BF[i]])
                P.op("pool", MSET(FB[i][:, L0 + L:W], 0.0), pwrites=[BF[i]])

            def wload(c):
                if c == 0 and "lru" in PRE:
                    return PRE.pop("lru")
                return (wchunk(wit_d[c, :, 4]), wchunk(wit_d[c, :, 5]), rgchunk(c))

            def front1(c, wts):
                (wx, Bwx) = wts[0]
                xr = free.pop(0)
                P.op("pool", MSET(FB[xr][:, C0 + LC:L0], 0.0), pwrites=[BF[xr]])
                for g in range(5):
                    bk = next_bank(LB)
                    n = proj_fm(wx, Bwx, g, bk)
                    off = C0 if g == 0 else L0 + (g - 1) * 512
                    P.op("act", ACTF(FB[xr][:, off:off + n], bank_ap[bk][:, 0:n], AF.Identity), reads=[bank_b[bk]], pwrites=[BF[xr]])
                return xr

            def front2(c, xr):
                xc = free.pop(0)
                cw = lambda j: smalls[:, 56 + j * 8 + c:56 + j * 8 + c + 1]
                cb = smalls[:, 88 + c:88 + c + 1]
                Bdg = buf("dg")
                for j in range(4):
                    P.op("dve", TS(dg[:, j * 128:(j + 1) * 128], ident_f[:, :], cw(j), None, ALU.mult), reads=[Bidf, Bsm], pwrites=[Bdg])
                for (p0, n) in blocks:
                    bk = next_bank(LB)
                    for j in range(4):
                        P.op("pe", MM(bank_ap[bk][:, 0:n], dg[:, j * 128:(j + 1) * 128], FB[xr][:, p0 - 1 + j:p0 - 1 + j + n], j == 0, j == 3),
                             reads=[Bdg, BF[xr]], pwrites=[bank_b[bk]])
                    P.op("act", ACTF(FB[xc][:, p0:p0 + n], bank_ap[bk][:, 0:n], AF.Identity, bias=cb), reads=[bank_b[bk], Bsm], pwrites=[BF[xc]])
                    P.op("dve", TS(XCB[:, p0:p0 + n], bank_ap[bk][:, 0:n], cb, None, ALU.add), reads=[bank_b[bk], Bsm], pwrites=[Bxcb])
                free.append(xr)
                return xc

            def gates(c, d, wts):
                rg, Brg = wts[2]
                col = d * 8 + c
                a, bb = free.pop(0), free.pop(0)
                for ax, dst, hb0 in ((0, a, C_HBA), (1, bb, C_HBX)):
                    gmat = rg[:, (d * 2 + ax) * 128:(d * 2 + ax + 1) * 128]
                    for (p0, n) in blocks:
                        bk = next_bank(LB)
                        P.op("pe", MM(bank_ap[bk][:, 0:n], gmat, XCB[:, p0:p0 + n], True, True), reads=[Brg, Bxcb], writes=[bank_b[bk]])
                        P.op("act", ACTF(FB[dst][:, p0:p0 + n], bank_ap[bk][:, 0:n], AF.Tanh, scale=0.5,
                                         bias=cst[:, hb0 + col:hb0 + col + 1]), reads=[bank_b[bk], Bcst], pwrites=[BF[dst]])
                cn = cst[:, C_CN + col:C_CN + col + 1]
                h1 = cst[:, C_H1 + col:C_H1 + col + 1]
                P.op("act", ACTF(MP[:, lo:hi], FB[a][:, lo:hi], AF.Exp, scale=cn, bias=cn), reads=[BF[a], Bcst], writes=Bmp)
                P.op("act", ACTF(FB[a][:, lo:hi], FB[a][:, lo:hi], AF.Exp, scale=h1, bias=h1), reads=[BF[a], Bcst], writes=[BF[a]])
                P.op("act", ACTF(MP[:, lo:hi], MP[:, lo:hi], AF.Sqrt, scale=-1.0, bias=1.0), reads=Bmp, writes=Bmp)
                return a, bb, None

            def dve_u(a, bb, m, xc):
                P.op("dve", STT(FB[bb][:, lo:hi], FB[bb][:, lo:hi], 1.0, MP[:, lo:hi], ALU.add, ALU.mult), reads=[BF[bb]] + Bmp, writes=[BF[bb]])
                P.op("dve", STT(FB[bb][:, lo:hi], FB[bb][:, lo:hi], 0.5, FB[xc][:, lo:hi], ALU.mult, ALU.mult), reads=[BF[bb], BF[xc]], writes=[BF[bb]])

            def dve_scan(d, a, bb):
                A_, H_ = FB[a], FB[bb]
                if d == 0:
                    P.op("dve", SCAN(H_[:, C0:C0 + LC], A_[:, C0:C0 + LC], H_[:, C0:C0 + LC], 0.0), reads=[BF[a], BF[bb]], writes=[BF[bb]])
                    P.op("dve", SCAN(H_[:, L0:L0 + L], A_[:, L0:L0 + L], H_[:, L0:L0 + L], H_[:, C0 + LC - 1:C0 + LC]),
                         reads=[BF[a], BF[bb]], writes=[BF[bb]])
                else:
                    P.op("dve", SCAN(H_[:, C0 + LC - 1:C0 - 1:-1], A_[:, C0 + LC - 1:C0 - 1:-1], H_[:, C0 + LC - 1:C0 - 1:-1], 0.0),
                         reads=[BF[a], BF[bb]], writes=[BF[bb]])
                    P.op("dve", SCAN(H_[:, L0 + L - 1:L0 - 1:-1], A_[:, L0 + L - 1:L0 - 1:-1], H_[:, L0 + L - 1:L0 - 1:-1], H_[:, C0:C0 + 1]),
                         reads=[BF[a], BF[bb]], writes=[BF[bb]])
                free.append(a)

            def tail_act(c, wts):
                wg, Bwg = wts[1]
                tg = free.pop(0)
                bks = []
                for g in range(1, 5):
                    bk = next_bank(LB)
                    proj_fm(wg, Bwg, g, bk)
                    o = L0 + (g - 1) * 512
                    P.op("act", ACTF(FB[tg][:, o:o + 512], bank_ap[bk], AF.Tanh, scale=0.5), reads=[bank_b[bk]], pwrites=[BF[tg]])
                    P.op("dve", STT(FB[tg][:, o:o + 512], FB[tg][:, o:o + 512], 1.0, bank_ap[bk], ALU.add, ALU.mult),
                         reads=[BF[tg], bank_b[bk]], pwrites=[BF[tg]])
                    bks.append(bk)
                return tg, bks

            def tail_dve(c, hf, tg, bks):
                for g in range(1, 5):
                    bk = bks[g - 1]
                    o = L0 + (g - 1) * 512
                    P.op("dve", STT(lruG[:, c * L + (g - 1) * 512:c * L + g * 512], FB[hf][:, o:o + 512], 0.5, FB[tg][:, o:o + 512],
                                    ALU.mult, ALU.mult), reads=[BF[hf], BF[tg]], writes=[buf("lruG%d_%d" % (c, g - 1))])
                free.append(tg)
                free.append(hf)

            wts = wload(0)
            xr = front1(0, wts)
            xc = front2(0, xr)
            for c in range(8):
                a0, b0, m0 = gates(c, 0, wts)
                wts_n = wload(c + 1) if c + 1 < 8 else None
                if wts_n is not None:
                    xr_n = front1(c + 1, wts_n)
                dve_u(a0, b0, m0, xc)
                dve_scan(0, a0, b0)
                a1, b1, m1 = gates(c, 1, wts)
                if wts_n is not None:
                    xc_n = front2(c + 1, xr_n)
                dve_u(a1, b1, m1, xc)
                free.append(xc)
                tg, bks = tail_act(c, wts)
                dve_scan(1, a1, b1)
                P.op("dve", TT(FB[b0][:, L0:L0 + L], FB[b0][:, L0:L0 + L], FB[b1][:, L0:L0 + L], ALU.add), reads=[BF[b0], BF[b1]], writes=[BF[b0]])
                free.append(b1)
                tail_dve(c, b0, tg, bks)
                wts = wts_n
                if wts_n is not None:
                    xc = xc_n

        def phase_attn(b):
            TAB = U[:, 0:4096]
            Ctab = TAB[:, 0:2048]
            Stab = TAB[:, 2048:4096]
            Btab = buf("tab")
            KT = [UB[:, 8192 + i * T:8192 + (i + 1) * T] for i in range(2)]
            VV = [UB[:, 12800 + i * 2340:12800 + (i + 1) * 2340] for i in range(2)]
            QT = [UB[:, 17480 + i * 512:17480 + (i + 1) * 512] for i in range(2)]
            SGA = [UB[:, 18504 + i * 512:18504 + (i + 1) * 512] for i in range(2)]
            PT = [UB[:, 19528 + i * 1024:19528 + (i + 1) * 1024] for i in range(2)]
            PT.append(U[:, 14080:14592].bitcast(BF16))
            QBC = [UB[:, 21576 + i * 512:21576 + (i + 1) * 512] for i in range(2)]
            ONB = [UB[:, 22600 + i * 512:22600 + (i + 1) * 512] for i in range(2)]
            OT = [U[:, 11904 + i * 512:11904 + (i + 1) * 512] for i in range(2)]
            RF = [U[:, 12928 + i * 512:12928 + (i + 1) * 512] for i in range(2)]
            TG = U[:, 14976:14976 + 256].bitcast(BF16)
            junk = U[:, 13952:13952 + 128]
            Bkt = [buf("KT%d" % i) for i in range(2)]
            Bvv = [buf("VV%d" % i) for i in range(2)]
            Bqt = [buf("QT%d" % i) for i in range(2)]
            Bsga = [buf("SGA%d" % i) for i in range(2)]
            Bpt = [buf("PT%d" % i) for i in range(3)]
            Bqbc = [buf("QBC%d" % i) for i in range(2)]
            Bonb = [buf("ONB%d" % i) for i in range(2)]
            Bot = [buf("OT%d" % i) for i in range(2)]
            Brf = [buf("RF%d" % i) for i in range(2)]
            Btg, Bjk, Bst = buf("TG"), buf("ajunk"), buf("astat")
            for i in range(4):
                P.dma("sp", DMA(TAB[:, i * 1024:(i + 1) * 1024], tabs_d[:, i * 1024:(i + 1) * 1024]), pwrites=[Btab], sem_buf=Btab)
            for i in range(2):
                vv3_ = VV[i].rearrange("p (k e) -> p k e", e=130)
                P.op("dve", MSET(vv3_[:, :, 128:129], 1.0), pwrites=[Bvv[i]])
                P.op("dve", MSET(vv3_[:, :, 129:130], 0.0), pwrites=[Bvv[i]])

            def rope1(bk, n, tok0, w):
                P.op("dve", TT(RF[w][:, 0:n], bank_ap[bk][:, 0:n], Ctab[:, tok0:tok0 + n], ALU.mult),
                     reads=[bank_b[bk], Btab], writes=[Brf[w]])
                if w == 1:
                    P.op("act", ACTF(QBC[w][:, 0:n], bank_ap[bk][:, 0:n], AF.Identity), reads=[bank_b[bk]], writes=[Bqbc[w]])
                else:
                    P.op("dve", CP(QBC[w][:, 0:n], bank_ap[bk][:, 0:n]), reads=[bank_b[bk]], writes=[Bqbc[w]])

            def rope2(bk2, n, tok0, w, dst_ap, Bdst):
                P.op("pe", MM(bank_ap[bk2][:, 0:n], perm_b[:, :], QBC[w][:, 0:n], True, True), reads=[Bpermb, Bqbc[w]], writes=[bank_b[bk2]])
                P.op("dve", TT(bank_ap[bk2][:, 0:n], bank_ap[bk2][:, 0:n], Stab[:, tok0:tok0 + n], ALU.mult),
                     reads=[bank_b[bk2], Btab], writes=[bank_b[bk2]])
                P.op("dve", TT(dst_ap, RF[w][:, 0:n], bank_ap[bk2][:, 0:n], ALU.add), reads=[Brf[w], bank_b[bk2]], pwrites=[Bdst])

            def q_stage1(wq, Bwq, qbn, bk):
                proj_fm(wq, Bwq, 1 + qbn, bk)
                rope1(bk, 512, qbn * 512, 0)

            def q_stage2(qbn, bk2):
                rope2(bk2, 512, qbn * 512, 0, QT[qbn % 2], Bqt[qbn % 2])

            def proj_part(w_ap, Bw, g, bk, k0, k1):
                for kc in range(k0, k1):
                    rhs, n = hT_blk(kc, g)
                    P.op("pe", MM(bank_ap[bk][:, 0:n], w_ap[:, kc * 128:(kc + 1) * 128], rhs, kc == 0, kc == 7),
                         reads=[Bw, BhT[g]], pwrites=[bank_b[bk]])

            def ga_tail(qbn, bk):
                P.op("act", ACTF(TG, bank_ap[bk], AF.Tanh, scale=0.5), reads=[bank_b[bk]], writes=[Btg])
                P.op("dve", STT(SGA[qbn % 2], TG, 1.0, bank_ap[bk], ALU.add, ALU.mult), reads=[Btg, bank_b[bk]], writes=[Bsga[qbn % 2]])

            def ga_stage(wga, Bwga, qbn, bk):
                proj_fm(wga, Bwga, 1 + qbn, bk)
                P.op("act", ACTF(TG, bank_ap[bk], AF.Tanh, scale=0.5), reads=[bank_b[bk]], writes=[Btg])
                P.op("dve", STT(SGA[qbn % 2], TG, 1.0, bank_ap[bk], ALU.add, ALU.mult), reads=[Btg, bank_b[bk]], writes=[Bsga[qbn % 2]])

            def epi_tail(h, qb):
                es = qb % 2
                for qt in range(4):
                    P.op("pe", TR(psb[:, 7 * 1024 + qt * 128:7 * 1024 + (qt + 1) * 128], ONB[es][:, qt * 128:(qt + 1) * 128], ident_b[:, :]),
                         reads=[Bonb[es], Bidb], pwrites=[bank_b[7]])
                P.op("dve", TT(attnG[:, h * L + qb * 512:h * L + (qb + 1) * 512], psb[:, 7 * 1024:7 * 1024 + 512], SGA[es], ALU.mult),
                     reads=[bank_b[7], Bsga[es]], writes=[buf("attnG%d_%d" % (h, qb))])

            def epi_head(qb):
                es = qb % 2
                for bkA, ncol in ((4, 3), (5, 3), (6, 2)):
                    srcs = bank_ap[bkA][:, 0:ncol * 130].rearrange("p (a e) -> p a e", e=130)[:, :, 128]
                    i0 = (bkA - 4) * 3
                    P.op("dve", RECIP(cst[:, C_RS + i0:C_RS + i0 + ncol], srcs), reads=[bank_b[bkA]], pwrites=[Bst])
                P.op("dve", TS(cst[:, C_RS + 4:C_RS + 8], cst[:, C_RS + 4:C_RS + 8], cst[:, C_LAM:C_LAM + 1], None, ALU.mult),
                     reads=[Bst, Bcst], writes=[Bst])
                for qt in range(4):
                    i0_, i1_ = qt, 4 + qt
                    a0 = bank_ap[4 + i0_ // 3][:, (i0_ % 3) * 130:(i0_ % 3) * 130 + 128]
                    a1 = bank_ap[4 + i1_ // 3][:, (i1_ % 3) * 130:(i1_ % 3) * 130 + 128]
                    o = OT[es][:, qt * 128:(qt + 1) * 128]
                    P.op("dve", TS(o, a0, cst[:, C_RS + i0_:C_RS + i0_ + 1], None, ALU.mult),
                         reads=[bank_b[4 + i0_ // 3], Bst], pwrites=[Bot[es]])
                    P.op("dve", STT(o, a1, cst[:, C_RS + i1_:C_RS + i1_ + 1], o, ALU.mult, ALU.add),
                         reads=[bank_b[4 + i1_ // 3], Bst, Bot[es]], pwrites=[Bot[es]])
                for qt in range(4):
                    o = OT[es][:, qt * 128:(qt + 1) * 128]
                    jk = junk if qt == 0 else U[:, 14592 + (qt - 1) * 128:14592 + qt * 128]
                    P.op("dve", lambda e, o=o, qt=qt, jk=jk: e.scalar_tensor_tensor(out=jk, in0=o, scalar=1.0, in1=o, op0=ALU.mult, op1=ALU.mult,
                                                                                    accum_out=cst[:, C_SQ4 + qt:C_SQ4 + qt + 1]),
                         reads=[Bot[es]], writes=[buf("ajunk%d" % qt)], pwrites=[buf("sq4")])
                P.op("pool", TS(cst[:, C_RS4:C_RS4 + 4], cst[:, C_SQ4:C_SQ4 + 4], 1.0 / 128, EPS, ALU.mult, ALU.add),
                     reads=[buf("sq4")], writes=[buf("rs4")])
                P.op("pool", TT(cst[:, C_RS4:C_RS4 + 4], cst[:, C_RS4:C_RS4 + 4], cst[:, C_NH:C_NH + 4], ALU.pow),
                     reads=[buf("rs4"), Bcst], writes=[buf("rs4")])
                for qt in range(4):
                    o = OT[es][:, qt * 128:(qt + 1) * 128]
                    P.op("dve", STT(ONB[es][:, qt * 128:(qt + 1) * 128], o, cst[:, C_RS4 + qt:C_RS4 + qt + 1], gsub_bc[:, :],
                                    ALU.mult, ALU.mult), reads=[Bot[es], buf("rs4"), Bgsub], pwrites=[Bonb[es]])

            def kv_proj(h, wk, Bwk, wv, Bwv):
                ks = h % 2
                vv3_ = VV[ks].rearrange("p (k e) -> p k e", e=130)

                def v_round(k4):
                    tiles = list(range(k4 * 4, min(18, k4 * 4 + 4)))
                    bk = 7
                    for ii, kt in enumerate(tiles):
                        g = 0 if kt < 2 else 1 + (kt - 2) // 4
                        for kc in range(8):
                            lhsT = hT[:, kc * T + kt * 128:kc * T + (kt + 1) * 128]
                            P.op("pe", MM(bank_ap[bk][:, ii * 128:(ii + 1) * 128], lhsT, wv[:, kc * 128:(kc + 1) * 128], kc == 0, kc == 7),
                                 reads=[Bwv, BhT[g]], pwrites=[bank_b[bk]])
                    nt = len(tiles)
                    src = bank_ap[bk][:, 0:nt * 128].rearrange("p (k e) -> p k e", e=128)
                    P.op("act", ACTF(vv3_[:, tiles[0]:tiles[0] + nt, 0:128], src, AF.Identity), reads=[bank_b[bk]], pwrites=[Bvv[ks]])

                def k_bank(g):
                    return (0, 2)[g % 2]

                proj_fm(wk, Bwk, 0, k_bank(0))
                P.op("act", ACTF(KT[ks][:, 0:LC], bank_ap[k_bank(0)][:, 0:LC], AF.Identity), reads=[bank_b[k_bank(0)]], pwrites=[Bkt[ks]])
                proj_fm(wk, Bwk, 1, k_bank(1))
                rope1(k_bank(1), 512, 0, 1)
                v_round(0)
                for g in range(1, 5):
                    rope2(k_bank(g) + 1, 512, (g - 1) * 512, 1, KT[ks][:, LC + (g - 1) * 512:LC + g * 512], Bkt[ks])
                    if g + 1 < 5:
                        proj_fm(wk, Bwk, g + 1, k_bank(g + 1))
                        rope1(k_bank(g + 1), 512, g * 512, 1)
                    v_round(g)

            def attention(h, qb, inject, defer_epi=False):
                ks, qs = h % 2, qb % 2
                vv3_ = VV[ks].rearrange("p (k e) -> p k e", e=130)

                def S_step(kt):
                    sbuf_i = kt % 2
                    for j in range(2):
                        bkS = sbuf_i * 2 + j
                        P.op("pe", MM(bank_ap[bkS], KT[ks][j * 64:(j + 1) * 64, kt * 128:(kt + 1) * 128],
                                      QT[qs][j * 64:(j + 1) * 64, :], True, True),
                             reads=[Bkt[ks], Bqt[qs]], writes=[bank_b[bkS]])

                def E_step(kt):
                    sbuf_i = kt % 2
                    P.op("act", ACTF(PT[kt % 3], ps[:, sbuf_i * 1024:(sbuf_i + 1) * 1024], AF.Exp, scale=0.125),
                         reads=[bank_b[sbuf_i * 2], bank_b[sbuf_i * 2 + 1]], writes=[Bpt[kt % 3]])

                def PV_step(kt):
                    sbuf_i = kt % 3
                    for j in range(2):
                        for qt in range(4):
                            idx = j * 4 + qt
                            bkA = 4 + idx // 3
                            c0 = (idx % 3) * 130
                            first = (kt == 0 and idx % 3 == 0)
                            P.op("pe", MM(bank_ap[bkA][:, c0:c0 + 130], PT[sbuf_i][:, j * 512 + qt * 128:j * 512 + (qt + 1) * 128],
                                          vv3_[:, kt, :], first, kt == 17, skip_group_check=True),
                                 reads=[Bpt[sbuf_i], Bvv[ks]], pwrites=[bank_b[bkA]])

                S_step(0)
                E_step(0)
                S_step(1)
                E_step(1)
                for fn in inject.get(-1, ()):
                    fn()
                for kt in range(18):
                    if kt + 2 < 18:
                        S_step(kt + 2)
                    PV_step(kt)
                    if kt + 2 < 18:
                        E_step(kt + 2)
                    for fn in inject.get(kt, ()):
                        fn()
                if not defer_epi:
                    epi_head(qb)

            W = {}

            casts = []

            def load_kv(h, defer=None):
                W[("k", h)] = wchunk(wit_d[h, :, 1], "dve", defer)
                W[("v", h)] = wchunk(wit_d[h, :, 2], "dve", defer)

            def load_qg(h, defer=None):
                W[("q", h)] = wchunk(wit_d[h, :, 0], "dve", defer)
                W[("g", h)] = wchunk(wit_d[h, :, 3], "dve", defer)

            def flush_casts():
                while casts:
                    casts.pop(0)()

            nh = DBG.get("nheads", NH)
            if "attn" in PRE:
                W.update(PRE.pop("attn"))
            else:
                load_kv(0)
                load_qg(0)
            pending_tail = None
            for h in range(nh):
                wk, Bwk = W[("k", h)]
                wv, Bwv = W[("v", h)]
                kv_proj(h, wk, Bwk, wv, Bwv)
                if h > 0:
                    epi_head(3)
                if h + 1 < nh:
                    load_kv(h + 1, casts)
                if h == 0:
                    wq, Bwq = W[("q", 0)]
                    wga, Bwga = W[("g", 0)]
                    q_stage1(wq, Bwq, 0, 0)
                    q_stage2(0, 1)
                    ga_stage(wga, Bwga, 0, 2)
                for qb in range(4):
                    inject = {}
                    if qb < 3:
                        nh_, nq_ = h, qb + 1
                    else:
                        nh_, nq_ = h + 1, 0
                    if nh_ < nh:
                        wq_, Bwq_ = W[("q", nh_)]
                        wga_, Bwga_ = W[("g", nh_)]
                        inject.setdefault(-1, []).append(
                            lambda wq_=wq_, Bwq_=Bwq_, nq_=nq_: proj_part(wq_, Bwq_, 1 + nq_, 7, 0, 8))
                        inject.setdefault(2, []).append(lambda nq_=nq_: rope1(7, 512, nq_ * 512, 0))
                        inject.setdefault(7, []).append(lambda nq_=nq_: q_stage2(nq_, 7))
                        for part in range(4):
                            inject.setdefault(11 + part, []).append(
                                lambda wga_=wga_, Bwga_=Bwga_, nq_=nq_, part=part: proj_part(wga_, Bwga_, 1 + nq_, 7, part * 2, part * 2 + 2))
                        inject.setdefault(14, []).append(lambda nq_=nq_: ga_tail(nq_, 7))
                    if pending_tail is not None:
                        ph, pq = pending_tail
                        inject.setdefault(9, []).append(lambda ph=ph, pq=pq: epi_tail(ph, pq))
                        pending_tail = None
                    if qb == 1 and h + 1 < nh:
                        load_qg(h + 1, casts)
                    if casts:
                        inject.setdefault(13, []).append(flush_casts)
                    last = (qb == 3 and h + 1 < nh)
                    attention(h, qb, inject, defer_epi=last)
                    pending_tail = (h, qb)
            if pending_tail is not None:
                epi_tail(*pending_tail)


        def phase_merge(b):
            ZT = UB[:, 0:8 * L]
            XRNG = [U[:, 8192 + i * 1024:8192 + (i + 1) * 1024] for i in range(2)]
            TMPA = [U[:, 8192 + i * 512:8192 + (i + 1) * 512] for i in range(4)]
            WOUT = UB[:, 2 * 10240:2 * 10240 + 8192]
            WOS = [U[:, 14336 + 0:14336 + 1024]]
            Bz = [[buf("ZT%d_%d" % (m, tb)) for tb in range(4)] for m in range(8)]
            Bxr_ = [buf("xrng%d" % i) for i in range(2)]
            Btmp = [buf("tmpa%d" % i) for i in range(4)]
            Bwout, Bwos = buf("wout"), buf("wos")
            Bjk2, Bsq1, Brs1 = buf("mjunk"), buf("msq"), buf("mrs")
            nxt = None
            for m in range(8):
                if nxt is None and "merge" in PRE:
                    ws = PRE.pop("merge")
                elif nxt is None:
                    ws = [wchunk(wgm_d[m]), wchunk(wgm_d[8 + m]), wchunk(wao_d[m]), wchunk(wlo_d[m])]
                else:
                    ws = nxt
                P.dma("sp", DMA(WOS[0], wout_d[:, m * 1024:(m + 1) * 1024]), writes=[Bwos], sem_buf=Bwos)
                P.op("pool", CP(WOUT[:, m * 1024:(m + 1) * 1024], WOS[0]), reads=[Bwos], pwrites=[Bwout])
                (wga_, Bga_), (wgl_, Bgl_), (wa_, Bwa_), (wl_, Bwl_) = ws
                for tb in range(4):
                    g = 1 + tb
                    bka, bkl, bga, bgl = 0, 1, 2, 3
                    if tb % 2 == 1:
                        bka, bkl, bga, bgl = 4, 5, 6, 7
                    for kc in range(8):
                        P.op("pe", MM(bank_ap[bga], wga_[:, kc * 128:(kc + 1) * 128], hT_blk(kc, g)[0], kc == 0, kc == 7),
                             reads=[Bga_, BhT[g]], pwrites=[bank_b[bga]])
                    for kc in range(8):
                        P.op("pe", MM(bank_ap[bgl], wgl_[:, kc * 128:(kc + 1) * 128], hT_blk(kc, g)[0], kc == 0, kc == 7),
                             reads=[Bgl_, BhT[g]], pwrites=[bank_b[bgl]])
                    for kc in range(8):
                        P.op("pe", MM(bank_ap[bka], wa_[:, kc * 128:(kc + 1) * 128], attnG[:, kc * L + tb * 512:kc * L + (tb + 1) * 512],
                                      kc == 0, kc == 7), reads=[Bwa_, buf("attnG%d_%d" % (kc, tb))], pwrites=[bank_b[bka]])
                    for kc in range(8):
                        P.op("pe", MM(bank_ap[bkl], wl_[:, kc * 128:(kc + 1) * 128], lruG[:, kc * L + tb * 512:kc * L + (tb + 1) * 512],
                                      kc == 0, kc == 7), reads=[Bwl_, buf("lruG%d_%d" % (kc, tb))], pwrites=[bank_b[bkl]])
                    ta, tl = TMPA[(tb % 2) * 2], TMPA[(tb % 2) * 2 + 1]
                    Bta, Btl = Btmp[(tb % 2) * 2], Btmp[(tb % 2) * 2 + 1]
                    P.op("act", ACTF(ta, bank_ap[bga], AF.Tanh, scale=0.5), reads=[bank_b[bga]], writes=[Bta])
                    P.op("act", ACTF(tl, bank_ap[bgl], AF.Tanh, scale=0.5), reads=[bank_b[bgl]], writes=[Btl])
                    P.op("dve", STT(ta, ta, 1.0, bank_ap[bka], ALU.add, ALU.mult), reads=[Bta, bank_b[bka]], writes=[Bta])
                    P.op("dve", STT(tl, tl, 1.0, bank_ap[bkl], ALU.add, ALU.mult), reads=[Btl, bank_b[bkl]], writes=[Btl])
                    P.op("dve", TT(ZT[:, m * L + tb * 512:m * L + (tb + 1) * 512], ta, tl, ALU.add), reads=[Bta, Btl], writes=[Bz[m][tb]])
                    if tb == 1 and m + 1 < 8:
                        nxt = [wchunk(wgm_d[m + 1]), wchunk(wgm_d[8 + m + 1])]
                if m + 1 < 8:
                    nxt = nxt + [wchunk(wao_d[m + 1]), wchunk(wlo_d[m + 1])]
            junk = UB[:, 2 * 14336:2 * 14336 + 1024]
            for tt in range(16):
                rs = tt % 2
                bk0 = (tt % 2) * 2
                P.dma("sp", DMA(XRNG[rs], x_d[b, tt * 128:(tt + 1) * 128, :]), writes=[Bxr_[rs]], pwrites=[Btmp[2 * rs], Btmp[2 * rs + 1]],
                      sem_buf=Bxr_[rs])
                for hf in range(2):
                    for kc in range(8):
                        P.op("pe", MM(bank_ap[bk0 + hf], ZT[:, kc * L + tt * 128:kc * L + (tt + 1) * 128],
                                      WOUT[:, kc * 1024 + hf * 512:kc * 1024 + (hf + 1) * 512], kc == 0, kc == 7),
                             reads=[Bz[kc][tt // 4], Bwout], pwrites=[bank_b[bk0 + hf]])
                y2 = ps[:, bk0 * 512:bk0 * 512 + 1024]
                P.op("act", ACTF(junk, y2, AF.Square, accum_out=cst[:, C_SSQ:C_SSQ + 1]),
                     reads=[bank_b[bk0], bank_b[bk0 + 1]], writes=[Bwos], pwrites=[Bsq1])
                P.op("pool", TS(cst[:, C_RSTD:C_RSTD + 1], cst[:, C_SSQ:C_SSQ + 1], 1.0 / D, 4.0 * EPS, ALU.mult, ALU.add),
                     reads=[Bsq1], writes=[Brs1])
                P.op("pool", TT(cst[:, C_RSTD:C_RSTD + 1], cst[:, C_RSTD:C_RSTD + 1], cst[:, C_NH:C_NH + 1], ALU.pow),
                     reads=[Brs1, Bcst], writes=[Brs1])
                P.op("dve", STT(y2, y2, cst[:, C_RSTD:C_RSTD + 1], Gbc[:, b * D:(b + 1) * D], ALU.mult, ALU.mult),
                     reads=[bank_b[bk0], bank_b[bk0 + 1], Brs1, BG[b]], writes=[bank_b[bk0], bank_b[bk0 + 1]])
                P.op("dve", TT(XRNG[rs], y2, XRNG[rs], ALU.add), reads=[bank_b[bk0], bank_b[bk0 + 1], Bxr_[rs]], writes=[Bxr_[rs]])
                P.dma("sp", DMA(out_d[b, tt * 128:(tt + 1) * 128, :], XRNG[rs]), reads=[Bxr_[rs]], sem_buf=Bxr_[rs])

        def dump(name, src, bufs):
            P.barrier()
            Bd = buf("dbg_" + name)
            P.dma("sp", DMA(dbg_d[name][:, :], src), reads=bufs, sem_buf=Bd)
            P.barrier()

        done = False
        for b in range(nb):
            phase_p1(b)
            PRE["lru"] = (wchunk(wit_d[0, :, 4]), wchunk(wit_d[0, :, 5]), rgchunk(0))
            P.barrier()
            if dbg and b == 0:
                dump("hT", hT[:, :], BhT)
            if stop_after == "p1":
                break
            phase_lru(b)
            PRE["attn"] = {("k", 0): wchunk(wit_d[0, :, 1]), ("v", 0): wchunk(wit_d[0, :, 2]),
                           ("q", 0): wchunk(wit_d[0, :, 0]), ("g", 0): wchunk(wit_d[0, :, 3])}
            P.barrier()
            if dbg and b == 0:
                dump("lruG", lruG[:, :], [])
            if stop_after == "lru":
                break
            phase_attn(b)
            PRE["merge"] = [wchunk(wgm_d[0]), wchunk(wgm_d[8]), wchunk(wao_d[0]), wchunk(wlo_d[0])]
            P.barrier()
            if dbg and b == 0:
                dump("attnG", attnG[:, :], [])
            if stop_after == "attn":
                break
            phase_merge(b)
            P.barrier()
        P.emit()
    return nc


def _rope_tables():
    t = np.arange(L)
    row = (t // 64).astype(np.float32)
    col = (t % 64).astype(np.float32)
    inv_freq = (np.float32(10000.0) ** (-(np.arange(0, 32, 2, dtype=np.float32)) / np.float32(32))).astype(np.float32)
    C = np.zeros((128, L), np.float32)
    S = np.zeros((128, L), np.float32)
    for p in range(128):
        half = (p % 64) // 32
        part = (p % 32) // 16
        i = p % 16
        ang = (row if half == 0 else col) * inv_freq[i]
        C[p] = np.cos(ang)
        S[p] = np.sin(ang) * (-1.0 if part == 0 else 1.0)
    return np.ascontiguousarray(np.concatenate([C, S], axis=1))


def _fm(v, n):
    return np.ascontiguousarray(np.asarray(v, np.float32).reshape(n, 128).T)


def prep_inputs(inp):
    f = lambda k: np.asarray(inp[k], np.float32)
    x, c, ctx, c_ctx = f("x"), f("c"), f("ctx"), f("c_ctx")
    w_mod, b_mod = f("w_mod")[0], f("b_mod")[0]
    g_pre, g_post = f("g_pre")[0], f("g_post")[0]
    w_in = f("w_in")[0]
    w5 = w_in.reshape(8, 128, 8, 8, 128)
    wit = np.ascontiguousarray(w5[:, :, 0:6].transpose(3, 1, 2, 0, 4)).reshape(8, 128, 6, 1024)
    wgm = np.ascontiguousarray(w5[:, :, 6:8].transpose(2, 3, 1, 0, 4)).reshape(16, 128, 1024)
    wao = np.ascontiguousarray(f("w_attn_out")[0].reshape(8, 128, 8, 128).transpose(2, 1, 0, 3)).reshape(8, 128, 1024)
    wlo = np.ascontiguousarray(f("w_lru_out")[0].reshape(8, 128, 8, 128).transpose(2, 1, 0, 3)).reshape(8, 128, 1024)
    wout = np.ascontiguousarray(f("w_out")[0].reshape(8, 128, 1024).transpose(1, 0, 2)).reshape(128, 8192)
    wmodA = np.ascontiguousarray(w_mod[:, :2048].reshape(8, 128, 16, 128).transpose(2, 1, 0, 3)).reshape(16, 128, 1024)
    wmodG = np.ascontiguousarray(w_mod[:, 2048:].reshape(8, 128, 2, 512).transpose(0, 2, 1, 3))
    w_rg_a, w_rg_x = f("w_rg_a")[0], f("w_rg_x")[0]
    wrg = np.zeros((8, 128, 4, 128), np.float32)
    for cch in range(8):
        for d in range(2):
            for ax, wsrc in ((0, w_rg_a), (1, w_rg_x)):
                for nl in range(2):
                    wrg[cch, nl * 64:(nl + 1) * 64, d * 2 + ax, nl * 64:(nl + 1) * 64] = wsrc[d, cch * 2 + nl]
    wrg = wrg.reshape(8, 128, 512)
    rows = np.concatenate([g_post, b_mod[2048:], f("g_subln")[0], f("lambda_q1")[0], f("lambda_k1")[0],
                           f("lambda_q2")[0], f("lambda_k2")[0]]).astype(np.float32)
    rows = np.ascontiguousarray(np.broadcast_to(rows[None, :], (128, NR)))
    conv_w, conv_b = f("conv_w")[0], f("conv_b")[0]
    b_rg_a, b_rg_x, lru_lambda = f("b_rg_a")[0], f("b_rg_x")[0], f("lru_lambda")[0]
    common = np.zeros((128, NS), np.float32)
    common[:, 32:48] = _fm(b_mod[:2048], 16)
    common[:, 48:56] = _fm(g_pre, 8)
    for j in range(4):
        common[:, 56 + j * 8:56 + (j + 1) * 8] = _fm(conv_w[j], 8)
    common[:, 88:96] = _fm(conv_b, 8)
    for d in range(2):
        common[:, 96 + d * 8:96 + (d + 1) * 8] = _fm(b_rg_a[d], 8)
        common[:, 112 + d * 8:112 + (d + 1) * 8] = _fm(b_rg_x[d], 8)
        common[:, 128 + d * 8:128 + (d + 1) * 8] = _fm(lru_lambda[d], 8)
    tabs = _rope_tables()
    ident = np.eye(128, dtype=np.float32)
    perm = np.zeros((128, 128), np.float32)
    for m_ in range(128):
        perm[m_ ^ 16, m_] = 1.0
    shared = dict(rows=rows, wmodA=wmodA, wmodG=wmodG, wit=wit, wrg=wrg, wgm=wgm, wao=wao, wlo=wlo, wout=wout,
                  tabs=tabs, ident=ident, perm=perm)
    maps = []
    for core in range(N_CORES):
        b0 = core * NB
        sm = common.copy()
        c4 = np.zeros((128, 8, 4), np.float32)
        for bb in range(NB):
            c4[:, :, bb] = _fm(c[b0 + bb], 8)
        c4[:, :, 2] = _fm(c_ctx, 8)
        sm[:, 0:32] = c4.reshape(128, 32)
        m = dict(shared)
        m["x"] = np.ascontiguousarray(x[b0:b0 + NB])
        m["ctx"] = np.ascontiguousarray(ctx[b0:b0 + NB])
        m["smalls"] = sm
        maps.append(m)
    return maps


_CACHE = {}


def kernel(**inputs):
    maps = prep_inputs(inputs)
    if "nc" not in _CACHE:
        _CACHE["nc"] = build_program()
    res = run_bass_kernel_spmd(_CACHE["nc"], maps, core_ids=list(range(N_CORES)))
    out = np.concatenate([np.asarray(r["out"], np.float32) for r in res.results], axis=0)
    return out
```

```python
import math
from contextlib import ExitStack

import numpy as np
import concourse.bass as bass
import concourse.mybir as mybir
from concourse.bass_utils import run_bass_kernel_spmd

F32 = mybir.dt.float32
BF16 = mybir.dt.bfloat16
ALU = mybir.AluOpType
AF = mybir.ActivationFunctionType
AX = mybir.AxisListType

N_CORES = 8
NB = 2
D = 1024
L = 2048
LC = 256
T = L + LC
NH = 8
EPS = 1e-6
LAM_INIT = 0.8 - 0.6 * math.exp(0.0)
NS = 144
NR = 2432
W = 2310
C0 = 1
L0 = 260
ENGS = ("pe", "act", "dve", "pool", "sp")
DBG = {}


class Buf:
    __slots__ = ("name", "writers", "readers", "dsem", "dcount", "psum")

    def __init__(self, name):
        self.name = name
        self.psum = name.startswith("bank")
        self.writers = []
        self.readers = []
        self.dsem = None
        self.dcount = 0


class Op:
    __slots__ = ("eng", "fn", "deps", "signal", "tok", "is_dma", "pos")

    def __init__(self, eng, fn, is_dma=False):
        self.eng = eng
        self.fn = fn
        self.deps = []
        self.signal = False
        self.tok = None
        self.is_dma = is_dma


class Prog:
    def __init__(self, nc, stack):
        self.nc = nc
        self.stack = stack
        self.ops = {e: [] for e in ENGS}
        self.esem = {e: stack.enter_context(nc.semaphore("s_" + e)) for e in ENGS if e != "sp"}
        self.nsig = {e: 0 for e in ENGS}
        self.dma_bufs = []
        self.bar_deps = {e: [] for e in ENGS}
        self.dma_since_bar = []

    def _track(self, op, reads, writes, pwrites):
        deps = []
        for b in reads:
            deps.extend(b.writers)
            if b.psum:
                deps.extend(r for r in b.readers if r.eng != op.eng)
            b.readers.append(op)
        for b in writes:
            deps.extend(b.readers)
            deps.extend(b.writers)
            b.writers = [op]
            b.readers = []
        for b in pwrites:
            if b.readers:
                deps.extend(b.readers)
                b.writers = [op]
                b.readers = []
            else:
                b.writers.append(op)
        if self.bar_deps[op.eng]:
            deps.extend(self.bar_deps[op.eng])
            self.bar_deps[op.eng] = []
        op.deps = [d for d in deps if d is not op]

    def op(self, eng, fn, reads=(), writes=(), pwrites=()):
        o = Op(eng, fn)
        self._track(o, reads, writes, pwrites)
        self.ops[eng].append(o)
        return o

    def dma(self, eng, fn, reads=(), writes=(), pwrites=(), sem_buf=None):
        o = Op(eng, fn, is_dma=True)
        self._track(o, reads, writes, pwrites)
        b = sem_buf
        if b.dsem is None:
            b.dsem = self.stack.enter_context(self.nc.semaphore("d_" + b.name))
            self.dma_bufs.append(b)
        b.dcount += 1
        o.tok = (b.dsem, 16 * b.dcount)
        o.signal = True
        self.ops[eng].append(o)
        self.dma_since_bar.append(o)
        return o

    def barrier(self):
        last = []
        for e in ENGS:
            for o in reversed(self.ops[e]):
                if not o.is_dma:
                    last.append(o)
                    break
        last.extend(self.dma_since_bar)
        self.dma_since_bar = []
        for e in ENGS:
            self.bar_deps[e] = self.bar_deps[e] + list(last)

    def finalize(self):
        if getattr(self, "_finalized", False):
            return
        self._finalized = True
        for e in ENGS:
            for i, o in enumerate(self.ops[e]):
                o.pos = i
        for e in ENGS:
            for o in self.ops[e]:
                best = {}
                keep = []
                for d in o.deps:
                    if d.is_dma:
                        keep.append(d)
                        continue
                    if d.eng == "pe" and o.eng == "pe":
                        continue
                    if d.eng not in best or best[d.eng].pos < d.pos:
                        best[d.eng] = d
                for d in best.values():
                    d.signal = True
                    keep.append(d)
                o.deps = keep
        for e in ENGS:
            if e == "sp":
                continue
            n = 0
            for o in self.ops[e]:
                if o.is_dma:
                    continue
                if o.signal:
                    n += 1
                    o.tok = (self.esem[e], n)
            self.nsig[e] = n

    def emit(self):
        self.finalize()
        prog = self

        def run(ename):
            def body(e):
                known = {}
                for o in prog.ops[ename]:
                    need = {}
                    for d in o.deps:
                        if d.eng == "pe" and ename == "pe" and not d.is_dma:
                            continue
                        sem, val = d.tok
                        k = id(sem)
                        if known.get(k, 0) >= val:
                            continue
                        if k not in need or need[k][1] < val:
                            need[k] = (sem, val)
                    for k, (sem, val) in need.items():
                        e.wait_ge(sem, val)
                        known[k] = val
                    ins = o.fn(e)
                    if o.signal:
                        ins.then_inc(o.tok[0], 16 if o.is_dma else 1)
                if ename == "sp":
                    for b in prog.dma_bufs:
                        if known.get(id(b.dsem), 0) < 16 * b.dcount:
                            e.wait_ge(b.dsem, 16 * b.dcount)
                    for en in ENGS:
                        if en != "sp" and prog.nsig[en] > 0:
                            e.wait_ge(prog.esem[en], prog.nsig[en])
            return body

        with self.nc.Block() as block:
            block.tensor(run("pe"))
            block.scalar(run("act"))
            block.vector(run("dve"))
            block.gpsimd(run("pool"))
            block.sync(run("sp"))


def MM(out, lhsT, rhs, start, stop, **kw):
    return lambda e: e.matmul(out, lhsT=lhsT, rhs=rhs, start=start, stop=stop, **kw)


def TR(out, in_, ident):
    return lambda e: e.transpose(out=out, in_=in_, identity=ident)


def ACTF(out, in_, func, scale=1.0, bias=0.0, accum_out=None):
    if accum_out is None:
        return lambda e: e.activation(out=out, in_=in_, func=func, bias=bias, scale=scale)
    return lambda e: e.activation(out=out, in_=in_, func=func, bias=bias, scale=scale, accum_out=accum_out)


def TS(out, in0, s1, s2, op0, op1=None):
    if op1 is None:
        return lambda e: e.tensor_scalar(out=out, in0=in0, scalar1=s1, scalar2=None, op0=op0)
    return lambda e: e.tensor_scalar(out=out, in0=in0, scalar1=s1, scalar2=s2, op0=op0, op1=op1)


def STT(out, in0, scalar, in1, op0, op1):
    return lambda e: e.scalar_tensor_tensor(out=out, in0=in0, scalar=scalar, in1=in1, op0=op0, op1=op1)


def TT(out, in0, in1, op):
    return lambda e: e.tensor_tensor(out=out, in0=in0, in1=in1, op=op)


def CP(out, in_):
    return lambda e: e.tensor_copy(out=out, in_=in_)


def MSET(ap, val):
    return lambda e: e.memset(ap, val)


def SCAN(out, d0, d1, init):
    return lambda e: e.tensor_tensor_scan(out=out, data0=d0, data1=d1, initial=init, op0=ALU.mult, op1=ALU.add)


def DMA(out, in_):
    return lambda e: e.dma_start(out=out, in_=in_)


def RECIP(out, in_):
    return lambda e: e.reciprocal(out=out, in_=in_)


def RED(out, in_):
    return lambda e: e.tensor_reduce(out=out, in_=in_, axis=AX.X, op=ALU.add)


def build_program(nb=NB, stop_after=None, dbg=False):
    nc = bass.Bass("TRN2", target_bir_lowering=False)
    dr = lambda n, s, k="ExternalInput", dt=F32: nc.dram_tensor(n, list(s), dt, kind=k).ap()
    x_d = dr("x", [NB, L, D])
    ctx_d = dr("ctx", [NB, LC, D])
    smalls_d = dr("smalls", [128, NS])
    rows_d = dr("rows", [128, NR])
    wmodA_d = dr("wmodA", [16, 128, 1024])
    wmodG_d = dr("wmodG", [8, 2, 128, 512])
    wit_d = dr("wit", [8, 128, 6, 1024])
    wrg_d = dr("wrg", [8, 128, 512])
    wgm_d = dr("wgm", [16, 128, 1024])
    wao_d = dr("wao", [8, 128, 1024])
    wlo_d = dr("wlo", [8, 128, 1024])
    wout_d = dr("wout", [128, 8192])
    tabs_d = dr("tabs", [128, 4096])
    ident_d = dr("ident", [128, 128])
    perm_d = dr("perm", [128, 128])
    out_d = dr("out", [NB, L, D], k="ExternalOutput")
    dbg_d = {}
    if dbg:
        dbg_d["hT"] = dr("dbg_hT", [128, 8 * T], k="ExternalOutput", dt=BF16)
        dbg_d["lruG"] = dr("dbg_lruG", [128, 8 * L], k="ExternalOutput", dt=BF16)
        dbg_d["attnG"] = dr("dbg_attnG", [128, 8 * L], k="ExternalOutput", dt=BF16)

    with ExitStack() as st:
        P = Prog(nc, st)
        sb = lambda n, s, d: st.enter_context(nc.sbuf_tensor("sb_" + n, list(s), d))

        hT = sb("hT", [128, 8 * T], BF16)
        attnG = sb("attnG", [128, 8 * L], BF16)
        lruG = sb("lruG", [128, 8 * L], BF16)
        Gbc = sb("Gbc", [128, NB * D], F32)
        smalls = sb("smalls", [128, NS], F32)
        cst = sb("cst", [128, 512], F32)
        ident_f = sb("ident_f", [128, 128], F32)
        ident_b = sb("ident_b", [128, 128], BF16)
        perm_b = sb("perm_b", [128, 128], BF16)
        gsub_bc = sb("gsub_bc", [128, 128], F32)
        wstg = sb("wstg", [128, 3 * 1024], F32)
        wbf = sb("wbf", [128, 6 * 1024], BF16)
        rgstg = sb("rgstg", [128, 2 * 512], F32)
        rgbf = sb("rgbf", [128, 2 * 512], BF16)
        U = sb("U", [128, 15360], F32)
        UB = U[:, :].bitcast(BF16)
        ps = st.enter_context(nc.psum_tensor("ps", [128, 4096], F32))
        psb = ps[:, :].bitcast(BF16)

        B = {}

        def buf(name):
            if name not in B:
                B[name] = Buf(name)
            return B[name]

        bank_ap = [ps[:, i * 512:(i + 1) * 512] for i in range(8)]
        bank_b = [buf("bank%d" % i) for i in range(8)]
        Bsm, Bcst, Bidf, Bidb, Bpermb, Bgsub = buf("smalls"), buf("cst"), buf("idf"), buf("idb"), buf("permb"), buf("gsub")
        BG = [buf("G%d" % b) for b in range(NB)]
        BhT = [buf("hT%d" % i) for i in range(5)]
        Bwstg = [buf("wstg%d" % i) for i in range(3)]
        Bwbf = [buf("wbf%d" % i) for i in range(6)]
        Brgs = [buf("rgs%d" % i) for i in range(2)]
        Brgb = [buf("rgb%d" % i) for i in range(2)]

        C_SC4 = 0
        C_TMP = 32
        C_MOD = 64
        C_A = 128
        C_LAM = 152
        C_CN = 160
        C_H1 = 176
        C_HBA = 192
        C_HBX = 208
        C_NH = 224
        C_SSQ = 232
        C_RSTD = 240
        C_E = 248
        C_RS = 264
        C_SQ4 = 272
        C_RS4 = 276
        C_S12 = 280

        wctr = {"s": 0, "b": 0, "rs": 0, "rb": 0}

        def wchunk(src_ap, cast_eng="pool", defer=None):
            s = wctr["s"] % 3
            wctr["s"] += 1
            j = wctr["b"] % 6
            wctr["b"] += 1
            stg = wstg[:, s * 1024:(s + 1) * 1024]
            dst = wbf[:, j * 1024:(j + 1) * 1024]
            P.dma("sp", DMA(stg, src_ap), writes=[Bwstg[s]], sem_buf=Bwstg[s])
            cast = lambda: P.op(cast_eng, CP(dst, stg), reads=[Bwstg[s]], writes=[Bwbf[j]])
            if defer is None:
                cast()
            else:
                defer.append(cast)
            return dst, Bwbf[j]

        def wstage(src_ap, n=1024):
            s = wctr["s"] % 3
            wctr["s"] += 1
            stg = wstg[:, s * 1024:s * 1024 + n]
            P.dma("sp", DMA(stg, src_ap), writes=[Bwstg[s]], sem_buf=Bwstg[s])
            return stg, Bwstg[s]

        def rgchunk(c):
            s = wctr["rs"] % 2
            wctr["rs"] += 1
            stg = rgstg[:, s * 512:(s + 1) * 512]
            dst = rgbf[:, s * 512:(s + 1) * 512]
            P.dma("sp", DMA(stg, wrg_d[c]), writes=[Brgs[s]], sem_buf=Brgs[s])
            P.op("pool", CP(dst, stg), reads=[Brgs[s]], writes=[Brgb[s]])
            return dst, Brgb[s]

        PRE = {}
        bank_rr = {"i": 0}

        def next_bank(choices=(0, 1, 2, 3, 7)):
            i = choices[bank_rr["i"] % len(choices)]
            bank_rr["i"] += 1
            return i

        evac_rr = {"i": 0}

        P.dma("sp", DMA(smalls[:, :], smalls_d[:, :]), writes=[Bsm], sem_buf=Bsm)
        P.dma("sp", DMA(ident_f[:, :], ident_d[:, :]), writes=[Bidf], sem_buf=Bidf)
        Brows = buf("rows")
        rows = U[:, 0:NR]
        P.dma("sp", DMA(rows, rows_d[:, :]), writes=[Brows], sem_buf=Brows)
        Bpermf = buf("permf")
        permf = U[:, NR:NR + 128]
        P.dma("sp", DMA(permf, perm_d[:, :]), writes=[Bpermf], sem_buf=Bpermf)
        P.op("dve", CP(ident_b[:, :], ident_f[:, :]), reads=[Bidf], writes=[Bidb])
        P.op("dve", CP(perm_b[:, :], permf), reads=[Bpermf], writes=[Bpermb])
        P.op("act", ACTF(cst[:, C_TMP:C_TMP + 32], smalls[:, 0:32], AF.Tanh, scale=0.5), reads=[Bsm], writes=[Bcst])
        P.op("dve", STT(cst[:, C_SC4:C_SC4 + 32], cst[:, C_TMP:C_TMP + 32], 1.0, smalls[:, 0:32], ALU.add, ALU.mult),
             reads=[Bsm, Bcst], writes=[Bcst])
        P.op("dve", TS(cst[:, C_SC4:C_SC4 + 32], cst[:, C_SC4:C_SC4 + 32], 0.5, None, ALU.mult), reads=[Bcst], writes=[Bcst])
        P.op("dve", MSET(cst[:, C_NH:C_NH + 8], -0.5), writes=[Bcst])
        P.op("act", ACTF(cst[:, C_E:C_E + 16], smalls[:, 128:144], AF.Exp, scale=-1.0), reads=[Bsm], writes=[Bcst])
        P.op("act", ACTF(cst[:, C_E:C_E + 16], cst[:, C_E:C_E + 16], AF.Ln, bias=1.0), reads=[Bcst], writes=[Bcst])
        P.op("dve", TS(cst[:, C_CN:C_CN + 16], cst[:, C_E:C_E + 16], -8.0, None, ALU.mult), reads=[Bcst], writes=[Bcst])
        P.op("dve", TS(cst[:, C_H1:C_H1 + 16], cst[:, C_E:C_E + 16], -4.0, None, ALU.mult), reads=[Bcst], writes=[Bcst])
        P.op("dve", TS(cst[:, C_HBA:C_HBA + 16], smalls[:, 96:112], 0.5, None, ALU.mult), reads=[Bsm], writes=[Bcst])
        P.op("dve", TS(cst[:, C_HBX:C_HBX + 16], smalls[:, 112:128], 0.5, None, ALU.mult), reads=[Bsm], writes=[Bcst])
        ltmp = U[:, NR + 128:NR + 128 + 128]
        Bltmp = buf("ltmp")
        P.op("dve", TT(ltmp[:, 0:64], rows[:, 2176:2240], rows[:, 2240:2304], ALU.mult), reads=[Brows], writes=[Bltmp])
        P.op("dve", TT(ltmp[:, 64:128], rows[:, 2304:2368], rows[:, 2368:2432], ALU.mult), reads=[Brows, Bltmp], writes=[Bltmp])
        P.op("dve", RED(cst[:, C_S12:C_S12 + 1], ltmp[:, 0:64]), reads=[Bltmp], writes=[Bcst])
        P.op("dve", RED(cst[:, C_S12 + 1:C_S12 + 2], ltmp[:, 64:128]), reads=[Bltmp, Bcst], writes=[Bcst])
        P.op("act", ACTF(cst[:, C_S12:C_S12 + 2], cst[:, C_S12:C_S12 + 2], AF.Exp), reads=[Bcst], writes=[Bcst])
        P.op("dve", TT(cst[:, C_LAM:C_LAM + 1], cst[:, C_S12 + 1:C_S12 + 2], cst[:, C_S12:C_S12 + 1], ALU.subtract),
             reads=[Bcst], writes=[Bcst])
        P.op("dve", TS(cst[:, C_LAM:C_LAM + 1], cst[:, C_LAM:C_LAM + 1], -LAM_INIT, None, ALU.add), reads=[Bcst], writes=[Bcst])
        P.op("dve", TS(gsub_bc[:, :], rows[:, 2048:2176], (1.0 - LAM_INIT) * 0.5, None, ALU.mult), reads=[Brows], writes=[Bgsub])

        bi = 7
        for ch in range(16):
            stg, Bs = wstage(wmodA_d[ch])
            for kc in range(8):
                P.op("pe", MM(bank_ap[bi][:, ch * 4:(ch + 1) * 4], stg[:, kc * 128:(kc + 1) * 128],
                              cst[:, C_SC4 + kc * 4:C_SC4 + kc * 4 + 4], kc == 0, kc == 7),
                     reads=[Bs, Bcst], pwrites=[bank_b[bi]])
        for v in range(3):
            src = bank_ap[bi][:, 0:64].rearrange("p (c v) -> p c v", v=4)[:, :, v]
            dst = cst[:, C_MOD:C_MOD + 64].rearrange("p (c v) -> p c v", v=4)[:, :, v]
            P.op("dve", TT(dst, src, smalls[:, 32:48], ALU.add), reads=[bank_b[bi], Bsm, Bcst], writes=[Bcst])
        modv = cst[:, C_MOD:C_MOD + 64].rearrange("p (c v) -> p c v", v=4)
        for v in range(3):
            P.op("dve", STT(cst[:, C_A + v * 8:C_A + v * 8 + 8], modv[:, 8:16, v], 1.0, smalls[:, 48:56], ALU.add, ALU.mult),
                 reads=[Bcst, Bsm], writes=[Bcst])

        def A_col(v, kc):
            return cst[:, C_A + v * 8 + kc:C_A + v * 8 + kc + 1]

        def Sh_col(v, kc):
            return cst[:, C_MOD + kc * 4 + v:C_MOD + kc * 4 + v + 1]

        ones_f = U[:, 2816:2944]
        Bones = buf("ones")
        P.op("dve", MSET(ones_f, 1.0), writes=[Bones])
        scb = U[:, 3072:3072 + 16 * 128]
        Bscb = buf("scb")
        for b in range(NB):
            for kc in range(8):
                P.op("dve", TS(scb[:, (b * 8 + kc) * 128:(b * 8 + kc + 1) * 128], ones_f,
                               cst[:, C_SC4 + kc * 4 + b:C_SC4 + kc * 4 + b + 1], None, ALU.mult),
                     reads=[Bones, Bcst], pwrites=[Bscb])
        for kc in range(8):
            for hf in range(2):
                stg, Bs = wstage(wmodG_d[kc, hf], n=512)
                for b in range(NB):
                    bk = b * 2 + hf
                    P.op("pe", MM(bank_ap[bk], scb[:, (b * 8 + kc) * 128:(b * 8 + kc + 1) * 128], stg, kc == 0, kc == 7),
                         reads=[Bs, Bscb], pwrites=[bank_b[bk]])
        for b in range(NB):
            for hf in range(2):
                bk = b * 2 + hf
                g = Gbc[:, b * D + hf * 512:b * D + (hf + 1) * 512]
                P.op("dve", TT(g, bank_ap[bk], rows[:, 1024 + hf * 512:1024 + (hf + 1) * 512], ALU.add),
                     reads=[bank_b[bk], Brows], pwrites=[BG[b]])
                P.op("dve", TT(g, g, rows[:, hf * 512:(hf + 1) * 512], ALU.mult), reads=[BG[b], Brows], pwrites=[BG[b]])

        P.barrier()

        def phase_p1(b):
            ring = [U[:, i * 1024:(i + 1) * 1024] for i in range(8)]
            Bring = [buf("ring%d" % i) for i in range(8)]
            junk = UB[:, 16384:16384 + 1024]
            Bjunk = buf("junk")
            Bssq, Brstd = buf("ssq"), buf("rstd")
            slot = 0
            for g in range(5):
                n = 2 if g == 0 else 4
                v = 2 if g == 0 else b
                tok0 = 0 if g == 0 else LC + (g - 1) * 512
                slots = []
                for i in range(n):
                    s = slot % 8
                    slot += 1
                    slots.append(s)
                    src = ctx_d[b, i * 128:(i + 1) * 128, :] if g == 0 else x_d[b, (g - 1) * 512 + i * 128:(g - 1) * 512 + (i + 1) * 128, :]
                    P.dma("sp", DMA(ring[s], src), writes=[Bring[s]], sem_buf=Bring[s])
                for i, s in enumerate(slots):
                    P.op("act", ACTF(junk, ring[s], AF.Square, accum_out=cst[:, C_SSQ + i:C_SSQ + i + 1]),
                         reads=[Bring[s]], writes=[Bjunk], pwrites=[Bssq])
                P.op("pool", TS(cst[:, C_RSTD:C_RSTD + n], cst[:, C_SSQ:C_SSQ + n], 1.0 / D, EPS, ALU.mult, ALU.add),
                     reads=[Bssq], writes=[Brstd])
                P.op("pool", TT(cst[:, C_RSTD:C_RSTD + n], cst[:, C_RSTD:C_RSTD + n], cst[:, C_NH:C_NH + n], ALU.pow),
                     reads=[Brstd, Bcst], writes=[Brstd])
                for i, s in enumerate(slots):
                    P.op("dve", TS(ring[s], ring[s], cst[:, C_RSTD + i:C_RSTD + i + 1], None, ALU.mult),
                         reads=[Brstd, Bring[s]], writes=[Bring[s]])
                for kc in range(8):
                    bk = next_bank()
                    for i, s in enumerate(slots):
                        P.op("pe", TR(bank_ap[bk][:, i * 128:(i + 1) * 128], ring[s][:, kc * 128:(kc + 1) * 128], ident_f[:, :]),
                             reads=[Bring[s], Bidf], pwrites=[bank_b[bk]])
                    dst = hT[:, kc * T + tok0:kc * T + tok0 + n * 128]
                    src = bank_ap[bk][:, 0:n * 128]
                    if evac_rr["i"] % 2 == 0:
                        P.op("act", ACTF(dst, src, AF.Identity, scale=A_col(v, kc), bias=Sh_col(v, kc)),
                             reads=[bank_b[bk], Bcst], pwrites=[BhT[g]])
                    else:
                        P.op("dve", TS(dst, src, A_col(v, kc), Sh_col(v, kc), ALU.mult, ALU.add),
                             reads=[bank_b[bk], Bcst], pwrites=[BhT[g]])
                    evac_rr["i"] += 1

        def hT_blk(kc, g):
            tok0 = 0 if g == 0 else LC + (g - 1) * 512
            n = 256 if g == 0 else 512
            return hT[:, kc * T + tok0:kc * T + tok0 + n], n

        def proj_fm(w_ap, Bw, g, bk):
            for kc in range(8):
                rhs, n = hT_blk(kc, g)
                P.op("pe", MM(bank_ap[bk][:, 0:n], w_ap[:, kc * 128:(kc + 1) * 128], rhs, kc == 0, kc == 7),
                     reads=[Bw, BhT[g]], pwrites=[bank_b[bk]])
            return n

        def phase_lru(b):
            FB = [U[:, i * W:(i + 1) * W] for i in range(6)]
            BF = [buf("LF%d" % i) for i in range(6)]
            XCB = UB[:, 12 * W:13 * W]
            Bxcb = buf("XCB")
            free = [0, 1, 2, 3, 4, 5]
            lo, hi = C0, L0 + L
            MP = ps[:, 2 * 512:2 * 512 + W]
            Bmp = [bank_b[i] for i in range(2, 7)]
            LB = (0, 1, 7)
            blocks = []
            p0 = lo
            while p0 < hi:
                blocks.append((p0, min(512, hi - p0)))
                p0 += 512
            for i in range(6):
                P.op("pool", MSET(FB[i][:, 0:1], 0.0), pwrites=[BF[i]])
                P.op("pool", MSET(FB[i][:, L0 + L:W], 0.0), pwrites=[BF[i]])

            def wload(c):
                if c == 0 and "lru" in PRE:
                    return PRE.pop("lru")
                return (wchunk(wit_d[c, :, 4]), wchunk(wit_d[c, :, 5]), rgchunk(c))

            def front1(c, wts):
                (wx, Bwx) = wts[0]
                xr = free.pop(0)
                P.op("pool", MSET(FB[xr][:, C0 + LC:L0], 0.0), pwrites=[BF[xr]])
                for g in range(5):
                    bk = next_bank(LB)
                    n = proj_fm(wx, Bwx, g, bk)
                    off = C0 if g == 0 else L0 + (g - 1) * 512
                    P.op("act", ACTF(FB[xr][:, off:off + n], bank_ap[bk][:, 0:n], AF.Identity), reads=[bank_b[bk]], pwrites=[BF[xr]])
                return xr

            def front2(c, xr):
                xc = free.pop(0)
                cw = lambda j: smalls[:, 56 + j * 8 + c:56 + j * 8 + c + 1]
                cb = smalls[:, 88 + c:88 + c + 1]
                P.op("dve", TS(FB[xc][:, lo:hi], FB[xr][:, lo - 1:hi - 1], cw(0), cb, ALU.mult, ALU.add), reads=[BF[xr], Bsm], writes=[BF[xc]])
                for j in range(1, 4):
                    P.op("dve", STT(FB[xc][:, lo:hi], FB[xr][:, lo - 1 + j:hi - 1 + j], cw(j), FB[xc][:, lo:hi], ALU.mult, ALU.add),
                         reads=[BF[xr], Bsm, BF[xc]], writes=[BF[xc]])
                free.append(xr)
                P.op("act", ACTF(XCB[:, lo:hi], FB[xc][:, lo:hi], AF.Identity), reads=[BF[xc]], writes=[Bxcb])
                return xc

            def gates(c, d, wts):
                rg, Brg = wts[2]
                col = d * 8 + c
                a, bb = free.pop(0), free.pop(0)
                for ax, dst, hb0 in ((0, a, C_HBA), (1, bb, C_HBX)):
                    gmat = rg[:, (d * 2 + ax) * 128:(d * 2 + ax + 1) * 128]
                    for (p0, n) in blocks:
                        bk = next_bank(LB)
                        P.op("pe", MM(bank_ap[bk][:, 0:n], gmat, XCB[:, p0:p0 + n], True, True), reads=[Brg, Bxcb], writes=[bank_b[bk]])
                        P.op("act", ACTF(FB[dst][:, p0:p0 + n], bank_ap[bk][:, 0:n], AF.Tanh, scale=0.5,
                                         bias=cst[:, hb0 + col:hb0 + col + 1]), reads=[bank_b[bk], Bcst], pwrites=[BF[dst]])
                cn = cst[:, C_CN + col:C_CN + col + 1]
                h1 = cst[:, C_H1 + col:C_H1 + col + 1]
                P.op("act", ACTF(MP[:, lo:hi], FB[a][:, lo:hi], AF.Exp, scale=cn, bias=cn), reads=[BF[a], Bcst], writes=Bmp)
                P.op("act", ACTF(FB[a][:, lo:hi], FB[a][:, lo:hi], AF.Exp, scale=h1, bias=h1), reads=[BF[a], Bcst], writes=[BF[a]])
                P.op("act", ACTF(MP[:, lo:hi], MP[:, lo:hi], AF.Sqrt, scale=-1.0, bias=1.0), reads=Bmp, writes=Bmp)
                return a, bb, None

            def dve_u(a, bb, m, xc):
                P.op("dve", STT(FB[bb][:, lo:hi], FB[bb][:, lo:hi], 1.0, MP[:, lo:hi], ALU.add, ALU.mult), reads=[BF[bb]] + Bmp, writes=[BF[bb]])
                P.op("dve", STT(FB[bb][:, lo:hi], FB[bb][:, lo:hi], 0.5, FB[xc][:, lo:hi], ALU.mult, ALU.mult), reads=[BF[bb], BF[xc]], writes=[BF[bb]])

            def dve_scan(d, a, bb):
                A_, H_ = FB[a], FB[bb]
                if d == 0:
                    P.op("dve", SCAN(H_[:, C0:C0 + LC], A_[:, C0:C0 + LC], H_[:, C0:C0 + LC], 0.0), reads=[BF[a], BF[bb]], writes=[BF[bb]])
                    P.op("dve", SCAN(H_[:, L0:L0 + L], A_[:, L0:L0 + L], H_[:, L0:L0 + L], H_[:, C0 + LC - 1:C0 + LC]),
                         reads=[BF[a], BF[bb]], writes=[BF[bb]])
                else:
                    P.op("dve", SCAN(H_[:, C0 + LC - 1:C0 - 1:-1], A_[:, C0 + LC - 1:C0 - 1:-1], H_[:, C0 + LC - 1:C0 - 1:-1], 0.0),
                         reads=[BF[a], BF[bb]], writes=[BF[bb]])
                    P.op("dve", SCAN(H_[:, L0 + L - 1:L0 - 1:-1], A_[:, L0 + L - 1:L0 - 1:-1], H_[:, L0 + L - 1:L0 - 1:-1], H_[:, C0:C0 + 1]),
                         reads=[BF[a], BF[bb]], writes=[BF[bb]])
                free.append(a)

            def tail_act(c, wts):
                wg, Bwg = wts[1]
                tg = free.pop(0)
                bks = []
                for g in range(1, 5):
                    bk = next_bank(LB)
                    proj_fm(wg, Bwg, g, bk)
                    o = L0 + (g - 1) * 512
                    P.op("act", ACTF(FB[tg][:, o:o + 512], bank_ap[bk], AF.Tanh, scale=0.5), reads=[bank_b[bk]], pwrites=[BF[tg]])
                    P.op("dve", STT(FB[tg][:, o:o + 512], FB[tg][:, o:o + 512], 1.0, bank_ap[bk], ALU.add, ALU.mult),
                         reads=[BF[tg], bank_b[bk]], pwrites=[BF[tg]])
                    bks.append(bk)
                return tg, bks

            def tail_dve(c, hf, tg, bks):
                for g in range(1, 5):
                    bk = bks[g - 1]
                    o = L0 + (g - 1) * 512
                    P.op("dve", STT(lruG[:, c * L + (g - 1) * 512:c * L + g * 512], FB[hf][:, o:o + 512], 0.5, FB[tg][:, o:o + 512],
                                    ALU.mult, ALU.mult), reads=[BF[hf], BF[tg]], writes=[buf("lruG%d_%d" % (c, g - 1))])
                free.append(tg)
                free.append(hf)

            wts = wload(0)
            xr = front1(0, wts)
            xc = front2(0, xr)
            for c in range(8):
                a0, b0, m0 = gates(c, 0, wts)
                wts_n = wload(c + 1) if c + 1 < 8 else None
                if wts_n is not None:
                    xr_n = front1(c + 1, wts_n)
                dve_u(a0, b0, m0, xc)
                dve_scan(0, a0, b0)
                a1, b1, m1 = gates(c, 1, wts)
                if wts_n is not None:
                    xc_n = front2(c + 1, xr_n)
                dve_u(a1, b1, m1, xc)
                free.append(xc)
                tg, bks = tail_act(c, wts)
                dve_scan(1, a1, b1)
                P.op("dve", TT(FB[b0][:, L0:L0 + L], FB[b0][:, L0:L0 + L], FB[b1][:, L0:L0 + L], ALU.add), reads=[BF[b0], BF[b1]], writes=[BF[b0]])
                free.append(b1)
                tail_dve(c, b0, tg, bks)
                wts = wts_n
                if wts_n is not None:
                    xc = xc_n

        def phase_attn(b):
            TAB = U[:, 0:4096]
            Ctab = TAB[:, 0:2048]
            Stab = TAB[:, 2048:4096]
            Btab = buf("tab")
            KT = [UB[:, 8192 + i * T:8192 + (i + 1) * T] for i in range(2)]
            VV = [UB[:, 12800 + i * 2340:12800 + (i + 1) * 2340] for i in range(2)]
            QT = [UB[:, 17480 + i * 512:17480 + (i + 1) * 512] for i in range(2)]
            SGA = [UB[:, 18504 + i * 512:18504 + (i + 1) * 512] for i in range(2)]
            PT = [UB[:, 19528 + i * 1024:19528 + (i + 1) * 1024] for i in range(2)]
            PT.append(U[:, 14080:14592].bitcast(BF16))
            QBC = [UB[:, 21576 + i * 512:21576 + (i + 1) * 512] for i in range(2)]
            ONB = [UB[:, 22600 + i * 512:22600 + (i + 1) * 512] for i in range(2)]
            OT = [U[:, 11904 + i * 512:11904 + (i + 1) * 512] for i in range(2)]
            RF = [U[:, 12928 + i * 512:12928 + (i + 1) * 512] for i in range(2)]
            TG = U[:, 14976:14976 + 256].bitcast(BF16)
            junk = U[:, 13952:13952 + 128]
            Bkt = [buf("KT%d" % i) for i in range(2)]
            Bvv = [buf("VV%d" % i) for i in range(2)]
            Bqt = [buf("QT%d" % i) for i in range(2)]
            Bsga = [buf("SGA%d" % i) for i in range(2)]
            Bpt = [buf("PT%d" % i) for i in range(3)]
            Bqbc = [buf("QBC%d" % i) for i in range(2)]
            Bonb = [buf("ONB%d" % i) for i in range(2)]
            Bot = [buf("OT%d" % i) for i in range(2)]
            Brf = [buf("RF%d" % i) for i in range(2)]
            Btg, Bjk, Bst = buf("TG"), buf("ajunk"), buf("astat")
            for i in range(4):
                P.dma("sp", DMA(TAB[:, i * 1024:(i + 1) * 1024], tabs_d[:, i * 1024:(i + 1) * 1024]), pwrites=[Btab], sem_buf=Btab)
            for i in range(2):
                vv3_ = VV[i].rearrange("p (k e) -> p k e", e=130)
                P.op("dve", MSET(vv3_[:, :, 128:129], 1.0), pwrites=[Bvv[i]])
                P.op("dve", MSET(vv3_[:, :, 129:130], 0.0), pwrites=[Bvv[i]])

            def rope1(bk, n, tok0, w):
                P.op("dve", TT(RF[w][:, 0:n], bank_ap[bk][:, 0:n], Ctab[:, tok0:tok0 + n], ALU.mult),
                     reads=[bank_b[bk], Btab], writes=[Brf[w]])
                if w == 1:
                    P.op("act", ACTF(QBC[w][:, 0:n], bank_ap[bk][:, 0:n], AF.Identity), reads=[bank_b[bk]], writes=[Bqbc[w]])
                else:
                    P.op("dve", CP(QBC[w][:, 0:n], bank_ap[bk][:, 0:n]), reads=[bank_b[bk]], writes=[Bqbc[w]])

            def rope2(bk2, n, tok0, w, dst_ap, Bdst):
                P.op("pe", MM(bank_ap[bk2][:, 0:n], perm_b[:, :], QBC[w][:, 0:n], True, True), reads=[Bpermb, Bqbc[w]], writes=[bank_b[bk2]])
                P.op("dve", TT(bank_ap[bk2][:, 0:n], bank_ap[bk2][:, 0:n], Stab[:, tok0:tok0 + n], ALU.mult),
                     reads=[bank_b[bk2], Btab], writes=[bank_b[bk2]])
                P.op("dve", TT(dst_ap, RF[w][:, 0:n], bank_ap[bk2][:, 0:n], ALU.add), reads=[Brf[w], bank_b[bk2]], pwrites=[Bdst])

            def q_stage1(wq, Bwq, qbn, bk):
                proj_fm(wq, Bwq, 1 + qbn, bk)
                rope1(bk, 512, qbn * 512, 0)

            def q_stage2(qbn, bk2):
                rope2(bk2, 512, qbn * 512, 0, QT[qbn % 2], Bqt[qbn % 2])

            def proj_part(w_ap, Bw, g, bk, k0, k1):
                for kc in range(k0, k1):
                    rhs, n = hT_blk(kc, g)
                    P.op("pe", MM(bank_ap[bk][:, 0:n], w_ap[:, kc * 128:(kc + 1) * 128], rhs, kc == 0, kc == 7),
                         reads=[Bw, BhT[g]], pwrites=[bank_b[bk]])

            def ga_tail(qbn, bk):
                P.op("act", ACTF(TG, bank_ap[bk], AF.Tanh, scale=0.5), reads=[bank_b[bk]], writes=[Btg])
                P.op("dve", STT(SGA[qbn % 2], TG, 1.0, bank_ap[bk], ALU.add, ALU.mult), reads=[Btg, bank_b[bk]], writes=[Bsga[qbn % 2]])

            def ga_stage(wga, Bwga, qbn, bk):
                proj_fm(wga, Bwga, 1 + qbn, bk)
                P.op("act", ACTF(TG, bank_ap[bk], AF.Tanh, scale=0.5), reads=[bank_b[bk]], writes=[Btg])
                P.op("dve", STT(SGA[qbn % 2], TG, 1.0, bank_ap[bk], ALU.add, ALU.mult), reads=[Btg, bank_b[bk]], writes=[Bsga[qbn % 2]])

            def epi_tail(h, qb):
                es = qb % 2
                for qt in range(4):
                    P.op("pe", TR(psb[:, 7 * 1024 + qt * 128:7 * 1024 + (qt + 1) * 128], ONB[es][:, qt * 128:(qt + 1) * 128], ident_b[:, :]),
                         reads=[Bonb[es], Bidb], pwrites=[bank_b[7]])
                P.op("dve", TT(attnG[:, h * L + qb * 512:h * L + (qb + 1) * 512], psb[:, 7 * 1024:7 * 1024 + 512], SGA[es], ALU.mult),
                     reads=[bank_b[7], Bsga[es]], writes=[buf("attnG%d_%d" % (h, qb))])

            def epi_head(qb):
                es = qb % 2
                for bkA, ncol in ((4, 3), (5, 3), (6, 2)):
                    srcs = bank_ap[bkA][:, 0:ncol * 130].rearrange("p (a e) -> p a e", e=130)[:, :, 128]
                    i0 = (bkA - 4) * 3
                    P.op("dve", RECIP(cst[:, C_RS + i0:C_RS + i0 + ncol], srcs), reads=[bank_b[bkA]], pwrites=[Bst])
                P.op("dve", TS(cst[:, C_RS + 4:C_RS + 8], cst[:, C_RS + 4:C_RS + 8], cst[:, C_LAM:C_LAM + 1], None, ALU.mult),
                     reads=[Bst, Bcst], writes=[Bst])
                for qt in range(4):
                    i0_, i1_ = qt, 4 + qt
                    a0 = bank_ap[4 + i0_ // 3][:, (i0_ % 3) * 130:(i0_ % 3) * 130 + 128]
                    a1 = bank_ap[4 + i1_ // 3][:, (i1_ % 3) * 130:(i1_ % 3) * 130 + 128]
                    o = OT[es][:, qt * 128:(qt + 1) * 128]
                    P.op("dve", TS(o, a0, cst[:, C_RS + i0_:C_RS + i0_ + 1], None, ALU.mult),
                         reads=[bank_b[4 + i0_ // 3], Bst], pwrites=[Bot[es]])
                    P.op("dve", STT(o, a1, cst[:, C_RS + i1_:C_RS + i1_ + 1], o, ALU.mult, ALU.add),
                         reads=[bank_b[4 + i1_ // 3], Bst, Bot[es]], pwrites=[Bot[es]])
                for qt in range(4):
                    o = OT[es][:, qt * 128:(qt + 1) * 128]
                    jk = junk if qt == 0 else U[:, 14592 + (qt - 1) * 128:14592 + qt * 128]
                    P.op("dve", lambda e, o=o, qt=qt, jk=jk: e.scalar_tensor_tensor(out=jk, in0=o, scalar=1.0, in1=o, op0=ALU.mult, op1=ALU.mult,
                                                                                    accum_out=cst[:, C_SQ4 + qt:C_SQ4 + qt + 1]),
                         reads=[Bot[es]], writes=[buf("ajunk%d" % qt)], pwrites=[buf("sq4")])
                P.op("pool", TS(cst[:, C_RS4:C_RS4 + 4], cst[:, C_SQ4:C_SQ4 + 4], 1.0 / 128, EPS, ALU.mult, ALU.add),
                     reads=[buf("sq4")], writes=[buf("rs4")])
                P.op("pool", TT(cst[:, C_RS4:C_RS4 + 4], cst[:, C_RS4:C_RS4 + 4], cst[:, C_NH:C_NH + 4], ALU.pow),
                     reads=[buf("rs4"), Bcst], writes=[buf("rs4")])
                for qt in range(4):
                    o = OT[es][:, qt * 128:(qt + 1) * 128]
                    P.op("dve", STT(ONB[es][:, qt * 128:(qt + 1) * 128], o, cst[:, C_RS4 + qt:C_RS4 + qt + 1], gsub_bc[:, :],
                                    ALU.mult, ALU.mult), reads=[Bot[es], buf("rs4"), Bgsub], pwrites=[Bonb[es]])

            def kv_proj(h, wk, Bwk, wv, Bwv):
                ks = h % 2
                vv3_ = VV[ks].rearrange("p (k e) -> p k e", e=130)

                def v_round(k4):
                    tiles = list(range(k4 * 4, min(18, k4 * 4 + 4)))
                    bk = 7
                    for ii, kt in enumerate(tiles):
                        g = 0 if kt < 2 else 1 + (kt - 2) // 4
                        for kc in range(8):
                            lhsT = hT[:, kc * T + kt * 128:kc * T + (kt + 1) * 128]
                            P.op("pe", MM(bank_ap[bk][:, ii * 128:(ii + 1) * 128], lhsT, wv[:, kc * 128:(kc + 1) * 128], kc == 0, kc == 7),
                                 reads=[Bwv, BhT[g]], pwrites=[bank_b[bk]])
                    nt = len(tiles)
                    src = bank_ap[bk][:, 0:nt * 128].rearrange("p (k e) -> p k e", e=128)
                    P.op("act", ACTF(vv3_[:, tiles[0]:tiles[0] + nt, 0:128], src, AF.Identity), reads=[bank_b[bk]], pwrites=[Bvv[ks]])

                def k_bank(g):
                    return (0, 2)[g % 2]

                proj_fm(wk, Bwk, 0, k_bank(0))
                P.op("act", ACTF(KT[ks][:, 0:LC], bank_ap[k_bank(0)][:, 0:LC], AF.Identity), reads=[bank_b[k_bank(0)]], pwrites=[Bkt[ks]])
                proj_fm(wk, Bwk, 1, k_bank(1))
                rope1(k_bank(1), 512, 0, 1)
                v_round(0)
                for g in range(1, 5):
                    rope2(k_bank(g) + 1, 512, (g - 1) * 512, 1, KT[ks][:, LC + (g - 1) * 512:LC + g * 512], Bkt[ks])
                    if g + 1 < 5:
                        proj_fm(wk, Bwk, g + 1, k_bank(g + 1))
                        rope1(k_bank(g + 1), 512, g * 512, 1)
                    v_round(g)

            def attention(h, qb, inject, defer_epi=False):
                ks, qs = h % 2, qb % 2
                vv3_ = VV[ks].rearrange("p (k e) -> p k e", e=130)

                def S_step(kt):
                    sbuf_i = kt % 2
                    for j in range(2):
                        bkS = sbuf_i * 2 + j
                        P.op("pe", MM(bank_ap[bkS], KT[ks][j * 64:(j + 1) * 64, kt * 128:(kt + 1) * 128],
                                      QT[qs][j * 64:(j + 1) * 64, :], True, True),
                             reads=[Bkt[ks], Bqt[qs]], writes=[bank_b[bkS]])

                def E_step(kt):
                    sbuf_i = kt % 2
                    P.op("act", ACTF(PT[kt % 3], ps[:, sbuf_i * 1024:(sbuf_i + 1) * 1024], AF.Exp, scale=0.125),
                         reads=[bank_b[sbuf_i * 2], bank_b[sbuf_i * 2 + 1]], writes=[Bpt[kt % 3]])

                def PV_step(kt):
                    sbuf_i = kt % 3
                    for j in range(2):
                        for qt in range(4):
                            idx = j * 4 + qt
                            bkA = 4 + idx // 3
                            c0 = (idx % 3) * 130
                            first = (kt == 0 and idx % 3 == 0)
                            P.op("pe", MM(bank_ap[bkA][:, c0:c0 + 130], PT[sbuf_i][:, j * 512 + qt * 128:j * 512 + (qt + 1) * 128],
                                          vv3_[:, kt, :], first, kt == 17, skip_group_check=True),
                                 reads=[Bpt[sbuf_i], Bvv[ks]], pwrites=[bank_b[bkA]])

                S_step(0)
                E_step(0)
                S_step(1)
                E_step(1)
                for fn in inject.get(-1, ()):
                    fn()
                for kt in range(18):
                    if kt + 2 < 18:
                        S_step(kt + 2)
                    PV_step(kt)
                    if kt + 2 < 18:
                        E_step(kt + 2)
                    for fn in inject.get(kt, ()):
                        fn()
                if not defer_epi:
                    epi_head(qb)

            W = {}

            casts = []

            def load_kv(h, defer=None):
                W[("k", h)] = wchunk(wit_d[h, :, 1], "dve", defer)
                W[("v", h)] = wchunk(wit_d[h, :, 2], "dve", defer)

            def load_qg(h, defer=None):
                W[("q", h)] = wchunk(wit_d[h, :, 0], "dve", defer)
                W[("g", h)] = wchunk(wit_d[h, :, 3], "dve", defer)

            def flush_casts():
                while casts:
                    casts.pop(0)()

            nh = DBG.get("nheads", NH)
            if "attn" in PRE:
                W.update(PRE.pop("attn"))
            else:
                load_kv(0)
                load_qg(0)
            pending_tail = None
            for h in range(nh):
                wk, Bwk = W[("k", h)]
                wv, Bwv = W[("v", h)]
                kv_proj(h, wk, Bwk, wv, Bwv)
                if h > 0:
                    epi_head(3)
                if h + 1 < nh:
                    load_kv(h + 1, casts)
                if h == 0:
                    wq, Bwq = W[("q", 0)]
                    wga, Bwga = W[("g", 0)]
                    q_stage1(wq, Bwq, 0, 0)
                    q_stage2(0, 1)
                    ga_stage(wga, Bwga, 0, 2)
                for qb in range(4):
                    inject = {}
                    if qb < 3:
                        nh_, nq_ = h, qb + 1
                    else:
                        nh_, nq_ = h + 1, 0
                    if nh_ < nh:
                        wq_, Bwq_ = W[("q", nh_)]
                        wga_, Bwga_ = W[("g", nh_)]
                        inject.setdefault(-1, []).append(
                            lambda wq_=wq_, Bwq_=Bwq_, nq_=nq_: proj_part(wq_, Bwq_, 1 + nq_, 7, 0, 8))
                        inject.setdefault(2, []).append(lambda nq_=nq_: rope1(7, 512, nq_ * 512, 0))
                        inject.setdefault(7, []).append(lambda nq_=nq_: q_stage2(nq_, 7))
                        for part in range(4):
                            inject.setdefault(11 + part, []).append(
                                lambda wga_=wga_, Bwga_=Bwga_, nq_=nq_, part=part: proj_part(wga_, Bwga_, 1 + nq_, 7, part * 2, part * 2 + 2))
                        inject.setdefault(14, []).append(lambda nq_=nq_: ga_tail(nq_, 7))
                    if pending_tail is not None:
                        ph, pq = pending_tail
                        inject.setdefault(9, []).append(lambda ph=ph, pq=pq: epi_tail(ph, pq))
                        pending_tail = None
                    if qb == 1 and h + 1 < nh:
                        load_qg(h + 1, casts)
                    if casts:
                        inject.setdefault(13, []).append(flush_casts)
                    last = (qb == 3 and h + 1 < nh)
                    attention(h, qb, inject, defer_epi=last)
                    pending_tail = (h, qb)
            if pending_tail is not None:
                epi_tail(*pending_tail)


        def phase_merge(b):
            ZT = UB[:, 0:8 * L]
            XRNG = [U[:, 8192 + i * 1024:8192 + (i + 1) * 1024] for i in range(2)]
            TMPA = [U[:, 8192 + i * 512:8192 + (i + 1) * 512] for i in range(4)]
            WOUT = UB[:, 2 * 10240:2 * 10240 + 8192]
            WOS = [U[:, 14336 + 0:14336 + 1024]]
            Bz = [[buf("ZT%d_%d" % (m, tb)) for tb in range(4)] for m in range(8)]
            Bxr_ = [buf("xrng%d" % i) for i in range(2)]
            Btmp = [buf("tmpa%d" % i) for i in range(4)]
            Bwout, Bwos = buf("wout"), buf("wos")
            Bjk2, Bsq1, Brs1 = buf("mjunk"), buf("msq"), buf("mrs")
            nxt = None
            for m in range(8):
                if nxt is None and "merge" in PRE:
                    ws = PRE.pop("merge")
                elif nxt is None:
                    ws = [wchunk(wgm_d[m]), wchunk(wgm_d[8 + m]), wchunk(wao_d[m]), wchunk(wlo_d[m])]
                else:
                    ws = nxt
                P.dma("sp", DMA(WOS[0], wout_d[:, m * 1024:(m + 1) * 1024]), writes=[Bwos], sem_buf=Bwos)
                P.op("pool", CP(WOUT[:, m * 1024:(m + 1) * 1024], WOS[0]), reads=[Bwos], pwrites=[Bwout])
                (wga_, Bga_), (wgl_, Bgl_), (wa_, Bwa_), (wl_, Bwl_) = ws
                for tb in range(4):
                    g = 1 + tb
                    bka, bkl, bga, bgl = 0, 1, 2, 3
                    if tb % 2 == 1:
                        bka, bkl, bga, bgl = 4, 5, 6, 7
                    for kc in range(8):
                        P.op("pe", MM(bank_ap[bga], wga_[:, kc * 128:(kc + 1) * 128], hT_blk(kc, g)[0], kc == 0, kc == 7),
                             reads=[Bga_, BhT[g]], pwrites=[bank_b[bga]])
                    for kc in range(8):
                        P.op("pe", MM(bank_ap[bgl], wgl_[:, kc * 128:(kc + 1) * 128], hT_blk(kc, g)[0], kc == 0, kc == 7),
                             reads=[Bgl_, BhT[g]], pwrites=[bank_b[bgl]])
                    for kc in range(8):
                        P.op("pe", MM(bank_ap[bka], wa_[:, kc * 128:(kc + 1) * 128], attnG[:, kc * L + tb * 512:kc * L + (tb + 1) * 512],
                                      kc == 0, kc == 7), reads=[Bwa_, buf("attnG%d_%d" % (kc, tb))], pwrites=[bank_b[bka]])
                    for kc in range(8):
                        P.op("pe", MM(bank_ap[bkl], wl_[:, kc * 128:(kc + 1) * 128], lruG[:, kc * L + tb * 512:kc * L + (tb + 1) * 512],
                                      kc == 0, kc == 7), reads=[Bwl_, buf("lruG%d_%d" % (kc, tb))], pwrites=[bank_b[bkl]])
                    ta, tl = TMPA[(tb % 2) * 2], TMPA[(tb % 2) * 2 + 1]
                    Bta, Btl = Btmp[(tb % 2) * 2], Btmp[(tb % 2) * 2 + 1]
                    P.op("act", ACTF(ta, bank_ap[bga], AF.Tanh, scale=0.5), reads=[bank_b[bga]], writes=[Bta])
                    P.op("act", ACTF(tl, bank_ap[bgl], AF.Tanh, scale=0.5), reads=[bank_b[bgl]], writes=[Btl])
                    P.op("dve", STT(ta, ta, 1.0, bank_ap[bka], ALU.add, ALU.mult), reads=[Bta, bank_b[bka]], writes=[Bta])
                    P.op("dve", STT(tl, tl, 1.0, bank_ap[bkl], ALU.add, ALU.mult), reads=[Btl, bank_b[bkl]], writes=[Btl])
                    P.op("dve", TT(ZT[:, m * L + tb * 512:m * L + (tb + 1) * 512], ta, tl, ALU.add), reads=[Bta, Btl], writes=[Bz[m][tb]])
                    if tb == 1 and m + 1 < 8:
                        nxt = [wchunk(wgm_d[m + 1]), wchunk(wgm_d[8 + m + 1])]
                if m + 1 < 8:
                    nxt = nxt + [wchunk(wao_d[m + 1]), wchunk(wlo_d[m + 1])]
            junk = UB[:, 2 * 14336:2 * 14336 + 1024]
            for tt in range(16):
                rs = tt % 2
                bk0 = (tt % 2) * 2
                P.dma("sp", DMA(XRNG[rs], x_d[b, tt * 128:(tt + 1) * 128, :]), writes=[Bxr_[rs]], pwrites=[Btmp[2 * rs], Btmp[2 * rs + 1]],
                      sem_buf=Bxr_[rs])
                for hf in range(2):
                    for kc in range(8):
                        P.op("pe", MM(bank_ap[bk0 + hf], ZT[:, kc * L + tt * 128:kc * L + (tt + 1) * 128],
                                      WOUT[:, kc * 1024 + hf * 512:kc * 1024 + (hf + 1) * 512], kc == 0, kc == 7),
                             reads=[Bz[kc][tt // 4], Bwout], pwrites=[bank_b[bk0 + hf]])
                y2 = ps[:, bk0 * 512:bk0 * 512 + 1024]
                P.op("act", ACTF(junk, y2, AF.Square, accum_out=cst[:, C_SSQ:C_SSQ + 1]),
                     reads=[bank_b[bk0], bank_b[bk0 + 1]], writes=[Bwos], pwrites=[Bsq1])
                P.op("pool", TS(cst[:, C_RSTD:C_RSTD + 1], cst[:, C_SSQ:C_SSQ + 1], 1.0 / D, 4.0 * EPS, ALU.mult, ALU.add),
                     reads=[Bsq1], writes=[Brs1])
                P.op("pool", TT(cst[:, C_RSTD:C_RSTD + 1], cst[:, C_RSTD:C_RSTD + 1], cst[:, C_NH:C_NH + 1], ALU.pow),
                     reads=[Brs1, Bcst], writes=[Brs1])
                P.op("dve", STT(y2, y2, cst[:, C_RSTD:C_RSTD + 1], Gbc[:, b * D:(b + 1) * D], ALU.mult, ALU.mult),
                     reads=[bank_b[bk0], bank_b[bk0 + 1], Brs1, BG[b]], writes=[bank_b[bk0], bank_b[bk0 + 1]])
                P.op("dve", TT(XRNG[rs], y2, XRNG[rs], ALU.add), reads=[bank_b[bk0], bank_b[bk0 + 1], Bxr_[rs]], writes=[Bxr_[rs]])
                P.dma("sp", DMA(out_d[b, tt * 128:(tt + 1) * 128, :], XRNG[rs]), reads=[Bxr_[rs]], sem_buf=Bxr_[rs])

        def dump(name, src, bufs):
            P.barrier()
            Bd = buf("dbg_" + name)
            P.dma("sp", DMA(dbg_d[name][:, :], src), reads=bufs, sem_buf=Bd)
            P.barrier()

        done = False
        for b in range(nb):
            phase_p1(b)
            PRE["lru"] = (wchunk(wit_d[0, :, 4]), wchunk(wit_d[0, :, 5]), rgchunk(0))
            P.barrier()
            if dbg and b == 0:
                dump("hT", hT[:, :], BhT)
            if stop_after == "p1":
                break
            phase_lru(b)
            PRE["attn"] = {("k", 0): wchunk(wit_d[0, :, 1]), ("v", 0): wchunk(wit_d[0, :, 2]),
                           ("q", 0): wchunk(wit_d[0, :, 0]), ("g", 0): wchunk(wit_d[0, :, 3])}
            P.barrier()
            if dbg and b == 0:
                dump("lruG", lruG[:, :], [])
            if stop_after == "lru":
                break
            phase_attn(b)
            PRE["merge"] = [wchunk(wgm_d[0]), wchunk(wgm_d[8]), wchunk(wao_d[0]), wchunk(wlo_d[0])]
            P.barrier()
            if dbg and b == 0:
                dump("attnG", attnG[:, :], [])
            if stop_after == "attn":
                break
            phase_merge(b)
            P.barrier()
        P.emit()
    return nc


def _rope_tables():
    t = np.arange(L)
    row = (t // 64).astype(np.float32)
    col = (t % 64).astype(np.float32)
    inv_freq = (np.float32(10000.0) ** (-(np.arange(0, 32, 2, dtype=np.float32)) / np.float32(32))).astype(np.float32)
    C = np.zeros((128, L), np.float32)
    S = np.zeros((128, L), np.float32)
    for p in range(128):
        half = (p % 64) // 32
        part = (p % 32) // 16
        i = p % 16
        ang = (row if half == 0 else col) * inv_freq[i]
        C[p] = np.cos(ang)
        S[p] = np.sin(ang) * (-1.0 if part == 0 else 1.0)
    return np.ascontiguousarray(np.concatenate([C, S], axis=1))


def _fm(v, n):
    return np.ascontiguousarray(np.asarray(v, np.float32).reshape(n, 128).T)


def prep_inputs(inp):
    f = lambda k: np.asarray(inp[k], np.float32)
    x, c, ctx, c_ctx = f("x"), f("c"), f("ctx"), f("c_ctx")
    w_mod, b_mod = f("w_mod")[0], f("b_mod")[0]
    g_pre, g_post = f("g_pre")[0], f("g_post")[0]
    w_in = f("w_in")[0]
    w5 = w_in.reshape(8, 128, 8, 8, 128)
    wit = np.ascontiguousarray(w5[:, :, 0:6].transpose(3, 1, 2, 0, 4)).reshape(8, 128, 6, 1024)
    wgm = np.ascontiguousarray(w5[:, :, 6:8].transpose(2, 3, 1, 0, 4)).reshape(16, 128, 1024)
    wao = np.ascontiguousarray(f("w_attn_out")[0].reshape(8, 128, 8, 128).transpose(2, 1, 0, 3)).reshape(8, 128, 1024)
    wlo = np.ascontiguousarray(f("w_lru_out")[0].reshape(8, 128, 8, 128).transpose(2, 1, 0, 3)).reshape(8, 128, 1024)
    wout = np.ascontiguousarray(f("w_out")[0].reshape(8, 128, 1024).transpose(1, 0, 2)).reshape(128, 8192)
    wmodA = np.ascontiguousarray(w_mod[:, :2048].reshape(8, 128, 16, 128).transpose(2, 1, 0, 3)).reshape(16, 128, 1024)
    wmodG = np.ascontiguousarray(w_mod[:, 2048:].reshape(8, 128, 2, 512).transpose(0, 2, 1, 3))
    w_rg_a, w_rg_x = f("w_rg_a")[0], f("w_rg_x")[0]
    wrg = np.zeros((8, 128, 4, 128), np.float32)
    for cch in range(8):
        for d in range(2):
            for ax, wsrc in ((0, w_rg_a), (1, w_rg_x)):
                for nl in range(2):
                    wrg[cch, nl * 64:(nl + 1) * 64, d * 2 + ax, nl * 64:(nl + 1) * 64] = wsrc[d, cch * 2 + nl]
    wrg = wrg.reshape(8, 128, 512)
    rows = np.concatenate([g_post, b_mod[2048:], f("g_subln")[0], f("lambda_q1")[0], f("lambda_k1")[0],
                           f("lambda_q2")[0], f("lambda_k2")[0]]).astype(np.float32)
    rows = np.ascontiguousarray(np.broadcast_to(rows[None, :], (128, NR)))
    conv_w, conv_b = f("conv_w")[0], f("conv_b")[0]
    b_rg_a, b_rg_x, lru_lambda = f("b_rg_a")[0], f("b_rg_x")[0], f("lru_lambda")[0]
    common = np.zeros((128, NS), np.float32)
    common[:, 32:48] = _fm(b_mod[:2048], 16)
    common[:, 48:56] = _fm(g_pre, 8)
    for j in range(4):
        common[:, 56 + j * 8:56 + (j + 1) * 8] = _fm(conv_w[j], 8)
    common[:, 88:96] = _fm(conv_b, 8)
    for d in range(2):
        common[:, 96 + d * 8:96 + (d + 1) * 8] = _fm(b_rg_a[d], 8)
        common[:, 112 + d * 8:112 + (d + 1) * 8] = _fm(b_rg_x[d], 8)
        common[:, 128 + d * 8:128 + (d + 1) * 8] = _fm(lru_lambda[d], 8)
    tabs = _rope_tables()
    ident = np.eye(128, dtype=np.float32)
    perm = np.zeros((128, 128), np.float32)
    for m_ in range(128):
        perm[m_ ^ 16, m_] = 1.0
    shared = dict(rows=rows, wmodA=wmodA, wmodG=wmodG, wit=wit, wrg=wrg, wgm=wgm, wao=wao, wlo=wlo, wout=wout,
                  tabs=tabs, ident=ident, perm=perm)
    maps = []
    for core in range(N_CORES):
        b0 = core * NB
        sm = common.copy()
        c4 = np.zeros((128, 8, 4), np.float32)
        for bb in range(NB):
            c4[:, :, bb] = _fm(c[b0 + bb], 8)
        c4[:, :, 2] = _fm(c_ctx, 8)
        sm[:, 0:32] = c4.reshape(128, 32)
        m = dict(shared)
        m["x"] = np.ascontiguousarray(x[b0:b0 + NB])
        m["ctx"] = np.ascontiguousarray(ctx[b0:b0 + NB])
        m["smalls"] = sm
        maps.append(m)
    return maps


_CACHE = {}


def kernel(**inputs):
    maps = prep_inputs(inputs)
    if "nc" not in _CACHE:
        _CACHE["nc"] = build_program()
    res = run_bass_kernel_spmd(_CACHE["nc"], maps, core_ids=list(range(N_CORES)))
    out = np.concatenate([np.asarray(r["out"], np.float32) for r in res.results], axis=0)
    return out
```
